# Optimizing a Trainium2 kernel written in Bass

```python
import math
import jax, jax.numpy as jnp
from jax import lax
import numpy as np

D_MODEL = 1024
BATCH = 16
SEQ = 2048
DEPTH = 4

N_MIXERS = 2
PLE_DIM = 256
EPS = 1e-6
NEG_INF = -1e30
H_A = 8
D_NOPE = 128
D_V = 128
D_CQ = 256
D_C = 256
H_I = 8
D_I = 128
TOPK_MAX = 256
Q_BLOCK = 128
N_BUCKETS = 32
MAX_DISTANCE = 128
A_WIDTH = H_A * D_V
A_IN = D_CQ + D_C + D_I + H_I + A_WIDTH
H_B = 8
D_K = 128
D_VB = 128
CONV_W = 4
CHUNK = 64
B_QKV = H_B * (2 * D_K + D_VB)
B_WIDTH = H_B * D_VB
B_IN = B_QKV + 2 * H_B + B_WIDTH
N_LAYERS_A = len(range(0, DEPTH, N_MIXERS))
N_LAYERS_B = len(range(1, DEPTH, N_MIXERS))

kernel_name = "hybrid_dsa_gated_deltanet_trunk"


def rms_norm(x, gain=None):
    xf = x.astype(jnp.float32)
    y = xf * lax.rsqrt(jnp.mean(xf * xf, axis=-1, keepdims=True) + EPS)
    if gain is not None:
        y = y * gain.astype(jnp.float32)
    return y.astype(x.dtype)


def l2_norm(x):
    return x * lax.rsqrt(jnp.sum(x * x, axis=-1, keepdims=True) + EPS)


def t5_bucket(rel):
    max_exact = N_BUCKETS // 2
    rel = jnp.maximum(rel, 0)
    rel_f = jnp.maximum(rel, 1).astype(jnp.float32)
    log_ratio = jnp.log(rel_f / max_exact) / math.log(MAX_DISTANCE / max_exact)
    large = max_exact + (log_ratio * (N_BUCKETS - max_exact)).astype(jnp.int32)
    large = jnp.minimum(large, N_BUCKETS - 1)
    return jnp.where(rel < max_exact, rel, large)


def causal_conv(x, conv_w):
    c = x.shape[-1]
    w = conv_w.shape[0]
    return lax.conv_general_dilated(
        x, conv_w[:, None, :].astype(x.dtype), window_strides=(1,), padding=[(w - 1, 0)],
        dimension_numbers=("NWC", "WIO", "NWC"), feature_group_count=c)


def sparse_latent_attention(q_lat, q_idx, w_idx, c_kv, k_idx, rel_bias, topk):
    b, s = q_lat.shape[0], q_lat.shape[1]
    nb = s // Q_BLOCK
    key_pos = jnp.arange(s, dtype=jnp.int32)

    def to_blocks(a):
        return jnp.moveaxis(a.reshape(b, nb, Q_BLOCK, *a.shape[2:]), 1, 0)

    def one_block(args):
        qb, qib, wb, start = args
        qpos = start + jnp.arange(Q_BLOCK, dtype=jnp.int32)
        idx_logits = jnp.einsum("bqhd,bsd->bqhs", qib, k_idx)
        score = jnp.einsum("bqhs,bqh->bqs", jax.nn.relu(idx_logits), wb).astype(jnp.float32)
        causal = key_pos[None, :] <= qpos[:, None]
        score = jnp.where(causal[None], score, NEG_INF)
        _, sel = lax.top_k(score, topk)
        valid = sel <= qpos[None, :, None]
        c_sel = jax.vmap(lambda c, i: c[i])(c_kv, sel)
        logits = jnp.einsum("bqhc,bqkc->bqhk", qb, c_sel).astype(jnp.float32)
        bias = rel_bias[t5_bucket(qpos[None, :, None] - sel)]
        logits = logits + jnp.moveaxis(bias, -1, 2).astype(jnp.float32)
        logits = jnp.where(valid[:, :, None, :], logits, NEG_INF)
        prob = jax.nn.softmax(logits, axis=-1).astype(c_sel.dtype)
        return jnp.einsum("bqhk,bqkc->bqhc", prob, c_sel)

    starts = jnp.arange(nb, dtype=jnp.int32) * Q_BLOCK
    out = lax.map(one_block, (to_blocks(q_lat), to_blocks(q_idx), to_blocks(w_idx), starts))
    return jnp.moveaxis(out, 0, 1).reshape(b, s, H_A, D_C)


def dsa_mixer(h, w_in, g_cq, w_uq, w_uk, g_q, g_kv, w_iq, w_uv, w_out, rel_bias):
    b, s, _ = h.shape
    proj = h @ w_in
    c_q, c_kv, k_idx, w_idx, z = jnp.split(
        proj, [D_CQ, D_CQ + D_C, D_CQ + D_C + D_I, D_CQ + D_C + D_I + H_I], axis=-1)
    c_q = rms_norm(c_q, g_cq)
    q_nope = jnp.einsum("bsc,chd->bshd", c_q, w_uq)
    q_lat = jnp.einsum("bshd,hdl->bshl", q_nope, w_uk)
    q_lat = rms_norm(q_lat, g_q) * (D_C ** -0.5)
    c_kv = rms_norm(c_kv, g_kv)
    q_idx = jnp.einsum("bsc,chd->bshd", c_q, w_iq) * (D_I ** -0.5)
    k_idx = rms_norm(k_idx)
    w_idx = w_idx * (H_I ** -0.5)
    topk = min(TOPK_MAX, s // 4)
    o_lat = sparse_latent_attention(q_lat, q_idx, w_idx, c_kv, k_idx, rel_bias, topk)
    o = jnp.einsum("bshl,hlv->bshv", o_lat, w_uv).reshape(b, s, A_WIDTH)
    return (o * jax.nn.silu(z)) @ w_out


def chunked_gated_delta_rule(q, k, v, g, beta):
    b, s, h, dk = q.shape
    dv = v.shape[-1]
    n = s // CHUNK

    def chunk4(a):
        return a.reshape(b, n, CHUNK, h, a.shape[-1]).transpose(0, 3, 1, 2, 4)

    qc, kc, vc = chunk4(q), chunk4(k), chunk4(v)
    gc = jnp.cumsum(g.reshape(b, n, CHUNK, h).transpose(0, 3, 1, 2), axis=-1)
    bc = beta.reshape(b, n, CHUNK, h).transpose(0, 3, 1, 2)
    t_idx = jnp.arange(CHUNK)
    incl = t_idx[:, None] >= t_idx[None, :]
    strict = t_idx[:, None] > t_idx[None, :]
    diff = gc[..., :, None] - gc[..., None, :]
    decay = jnp.where(incl, jnp.exp(jnp.where(incl, diff, 0.0)), 0.0)
    kb = kc * bc[..., None]
    lower = jnp.where(strict, jnp.einsum("bhntd,bhnjd->bhntj", kb, kc) * decay, 0.0)
    a_mat = lower + jnp.eye(CHUNK, dtype=jnp.float32)
    rhs = jnp.concatenate([vc * bc[..., None], kb * jnp.exp(gc)[..., None]], axis=-1)
    sol = lax.linalg.triangular_solve(a_mat, rhs, left_side=True, lower=True)
    u_pre, w_dec = sol[..., :dv], sol[..., dv:]
    aqk = jnp.where(incl, jnp.einsum("bhntd,bhnjd->bhntj", qc, kc) * decay, 0.0)
    q_dec = qc * jnp.exp(gc)[..., None]
    k_dec = kc * jnp.exp(gc[..., -1:] - gc)[..., None]
    c_dec = jnp.exp(gc[..., -1])

    def step(state, xs):
        u_p, w_d, a_qk, q_d, k_d, cd = xs
        u = u_p - jnp.einsum("bhck,bhkv->bhcv", w_d, state)
        o = jnp.einsum("bhck,bhkv->bhcv", q_d, state) + jnp.einsum("bhct,bhtv->bhcv", a_qk, u)
        state = state * cd[..., None, None] + jnp.einsum("bhck,bhcv->bhkv", k_d, u)
        return state, o

    xs = tuple(jnp.moveaxis(a, 2, 0) for a in (u_pre, w_dec, aqk, q_dec, k_dec, c_dec))
    state0 = jnp.zeros((b, h, dk, dv), jnp.float32)
    _, o = lax.scan(step, state0, xs)
    return o.transpose(1, 0, 3, 2, 4).reshape(b, s, h, dv)


def gdn_mixer(h, w_in, conv_w, a_log, dt_bias, g_o, w_out):
    b, s, _ = h.shape
    proj = h @ w_in
    qkv, beta_raw, a_raw, z = jnp.split(proj, [B_QKV, B_QKV + H_B, B_QKV + 2 * H_B], axis=-1)
    qkv = jax.nn.silu(causal_conv(qkv, conv_w))
    q, k, v = jnp.split(qkv, [H_B * D_K, 2 * H_B * D_K], axis=-1)
    q = l2_norm(q.reshape(b, s, H_B, D_K).astype(jnp.float32)) * (D_K ** -0.5)
    k = l2_norm(k.reshape(b, s, H_B, D_K).astype(jnp.float32))
    v = v.reshape(b, s, H_B, D_VB).astype(jnp.float32)
    beta = jax.nn.sigmoid(beta_raw.astype(jnp.float32))
    g = -jnp.exp(a_log.astype(jnp.float32)) * jax.nn.softplus(
        a_raw.astype(jnp.float32) + dt_bias.astype(jnp.float32))
    o = chunked_gated_delta_rule(q, k, v, g, beta).astype(h.dtype)
    o = rms_norm(o, g_o).reshape(b, s, B_WIDTH)
    return (o * jax.nn.silu(z)) @ w_out


def setup_inputs(seed: int = 0) -> dict:
    key = jax.random.key(seed)
    ks = jax.random.split(key, 24)
    f32 = jnp.float32

    def dense(k, shape, fan_in):
        return jax.random.normal(k, shape, f32) * (fan_in ** -0.5)

    def gain(k, shape):
        return 1.0 + 0.05 * jax.random.normal(k, shape, f32)

    dt = jnp.exp(jax.random.uniform(ks[17], (N_LAYERS_B, H_B), f32, math.log(1e-3), math.log(1e-1)))
    return {
        "x": jax.random.normal(ks[0], (BATCH, SEQ, D_MODEL), f32),
        "p": jax.random.normal(ks[1], (DEPTH, BATCH, SEQ, PLE_DIM), f32),
        "norm_w": gain(ks[2], (DEPTH, D_MODEL)),
        "a_w_in": dense(ks[3], (N_LAYERS_A, D_MODEL, A_IN), D_MODEL),
        "a_g_cq": gain(ks[4], (N_LAYERS_A, D_CQ)),
        "a_w_uq": dense(ks[5], (N_LAYERS_A, D_CQ, H_A, D_NOPE), D_CQ),
        "a_w_uk": dense(ks[6], (N_LAYERS_A, H_A, D_NOPE, D_C), D_NOPE),
        "a_g_q": gain(ks[7], (N_LAYERS_A, D_C)),
        "a_g_kv": gain(ks[8], (N_LAYERS_A, D_C)),
        "a_w_iq": dense(ks[9], (N_LAYERS_A, D_CQ, H_I, D_I), D_CQ),
        "a_w_uv": dense(ks[10], (N_LAYERS_A, H_A, D_C, D_V), D_C),
        "a_w_out": dense(ks[11], (N_LAYERS_A, A_WIDTH, D_MODEL), A_WIDTH),
        "rel_bias": 0.5 * jax.random.normal(ks[12], (N_BUCKETS, H_A), f32),
        "b_w_in": dense(ks[13], (N_LAYERS_B, D_MODEL, B_IN), D_MODEL),
        "b_conv_w": dense(ks[14], (N_LAYERS_B, CONV_W, B_QKV), CONV_W),
        "b_a_log": jnp.log(jax.random.uniform(ks[15], (N_LAYERS_B, H_B), f32, 1.0, 16.0)),
        "b_dt_bias": dt + jnp.log(-jnp.expm1(-dt)),
        "b_g_o": gain(ks[16], (N_LAYERS_B, D_VB)),
        "b_w_out": dense(ks[18], (N_LAYERS_B, B_WIDTH, D_MODEL), B_WIDTH),
        "ple_norm": gain(ks[19], (DEPTH, D_MODEL)),
        "ple_w_gate": dense(ks[20], (DEPTH, D_MODEL, D_MODEL), D_MODEL),
        "ple_w_proj": dense(ks[21], (DEPTH, PLE_DIM, D_MODEL), PLE_DIM),
    }


def reference(x, p, norm_w, a_w_in, a_g_cq, a_w_uq, a_w_uk, a_g_q, a_g_kv, a_w_iq, a_w_uv,
              a_w_out, rel_bias, b_w_in, b_conv_w, b_a_log, b_dt_bias, b_g_o, b_w_out,
              ple_norm, ple_w_gate, ple_w_proj):
    for i in range(DEPTH):
        h = rms_norm(x, norm_w[i])
        j = i // N_MIXERS
        if i % N_MIXERS == 0:
            y = dsa_mixer(h, a_w_in[j], a_g_cq[j], a_w_uq[j], a_w_uk[j], a_g_q[j], a_g_kv[j],
                          a_w_iq[j], a_w_uv[j], a_w_out[j], rel_bias)
        else:
            y = gdn_mixer(h, b_w_in[j], b_conv_w[j], b_a_log[j], b_dt_bias[j], b_g_o[j], b_w_out[j])
        x = x + y
        gate = jax.nn.sigmoid(rms_norm(x, ple_norm[i]) @ ple_w_gate[i])
        x = x + gate * (p[i] @ ple_w_proj[i])
    return x
```

```python
import math
import os
from contextlib import ExitStack
import numpy as np
import concourse.bass as bass
import concourse.mybir as mybir
from concourse.bass_utils import run_bass_kernel_spmd

F32 = mybir.dt.float32
BF16 = mybir.dt.bfloat16
ALU = mybir.AluOpType
AF = mybir.ActivationFunctionType

PE, DVE, ACT, POOL, SP = "tensor", "vector", "scalar", "gpsimd", "sync"
ENGS = (PE, DVE, ACT, POOL, SP)

D = 1024
SEQ = 2048
NEG = -1.0e30
EPS = 1e-6


class Prog:
    def __init__(self, nc):
        self.nc = nc
        self.streams = {e: [] for e in ENGS}
        self.count = {e: 0 for e in ENGS}
        self.sem = {e: nc.alloc_semaphore("c_" + e) for e in ENGS}
        self.seen = {e: {} for e in ENGS}
        self.state = {}
        self.dma_sems = {}
        self.n_ops = 0

    def _deps(self, reads, writes):
        deps = []
        for k in reads:
            st = self.state.get(k)
            if st and st[0] is not None:
                deps.append((st[0], True))
        for k in writes:
            st = self.state.get(k)
            if st:
                if st[0] is not None:
                    deps.append((st[0], True))
                for ev in st[1].values():
                    deps.append((ev, False))
        return deps

    def _waits(self, eng, deps):
        best = {}
        for (kind, key, val), raw in deps:
            if kind == "eng" and key == eng and (not raw or eng == PE):
                continue
            sid = (kind, key)
            if val <= self.seen[eng].get(sid, 0):
                continue
            if sid not in best or best[sid] < val:
                best[sid] = val
        for sid, val in best.items():
            self.seen[eng][sid] = val
        return [(k[0], k[1], v) for k, v in best.items()]

    def _commit(self, ev, reads, writes):
        src = (ev[0], ev[1])
        for k in reads:
            st = self.state.setdefault(k, [None, {}])
            st[1][src] = ev
        for k in writes:
            self.state[k] = [ev, {}]

    def op(self, eng, fn, reads=(), writes=()):
        rec = _Rec()
        fn(rec)
        name_, a_, kw_ = rec.call
        fn = lambda e, name_=name_, a_=a_, kw_=kw_: getattr(e, name_)(*a_, **kw_)
        waits = self._waits(eng, self._deps(reads, writes))
        self.count[eng] += 1
        ev = ("eng", eng, self.count[eng])
        self.streams[eng].append((waits, fn, ev))
        self._commit(ev, reads, writes)
        self.n_ops += 1
        return ev

    def _dma(self, eng, out, in_, reads, writes, semkey, kw):
        if semkey not in self.dma_sems:
            self.dma_sems[semkey] = [self.nc.alloc_semaphore("d%d" % len(self.dma_sems)), 0]
        ent = self.dma_sems[semkey]
        waits = self._waits(eng, self._deps(reads, writes))
        ent[1] += 16
        ev = ("dma", semkey, ent[1])
        self.streams[eng].append(
            (waits, lambda e, o=out, i=in_, kw=kw: e.dma_start(out=o, in_=i, **kw), ev))
        self._commit(ev, reads, writes)
        self.n_ops += 1
        return ev

    def load(self, dst, src, sb, dram=(), eng=SP, **kw):
        return self._dma(eng, dst, src, list(dram), [sb], sb, kw)

    def store(self, dst, src, sb, dram=(), eng=POOL, **kw):
        return self._dma(eng, dst, src, [sb], list(dram), sb, kw)

    def wait_all(self, eng, events):
        waits = self._waits(eng, [(ev, True) for ev in events])
        if waits:
            self.streams[eng].append((waits, None, None))

    def barrier(self):
        evs = [("eng", e, self.count[e]) for e in ENGS if self.count[e] > 0]
        evs += [("dma", k, v) for k, (s, v) in self.dma_sems.items() if v > 0]
        for e in ENGS:
            self.wait_all(e, evs)
        self.state = {}

    def emit(self):
        nc = self.nc
        targets = {e: set() for e in ENGS}
        for e in ENGS:
            for waits, fn, ev in self.streams[e]:
                for kind, key, val in waits:
                    if kind == "eng":
                        targets[key].add(val)
        rank = {e: {v: i + 1 for i, v in enumerate(sorted(targets[e]))} for e in ENGS}
        with nc.Block() as block:
            def run(engine_obj, name):
                for waits, fn, ev in self.streams[name]:
                    for kind, key, val in waits:
                        if kind == "eng":
                            engine_obj.wait_ge(self.sem[key], rank[key][val])
                        else:
                            engine_obj.wait_ge(self.dma_sems[key][0], val)
                    if fn is None:
                        continue
                    ins = fn(engine_obj)
                    if ev[0] == "dma":
                        ins.then_inc(self.dma_sems[ev[1]][0], 16)
                    elif ev[2] in targets[name]:
                        ins.then_inc(self.sem[name], 1)

            @block.tensor
            def _(e):
                run(e, PE)

            @block.vector
            def _(e):
                run(e, DVE)

            @block.scalar
            def _(e):
                run(e, ACT)

            @block.gpsimd
            def _(e):
                run(e, POOL)

            @block.sync
            def _(e):
                run(e, SP)


class _Rec:
    def __init__(self):
        self.call = None

    def __getattr__(self, name):
        def f(*a, **kw):
            self.call = (name, a, kw)
            return self
        return f


class Ring:
    def __init__(self, tiles, name):
        self.tiles = tiles
        self.name = name
        self.i = -1

    def next(self):
        self.i = (self.i + 1) % len(self.tiles)
        return self.tiles[self.i], (self.name, self.i)


def _t5_bucket(rel):
    rel = np.maximum(rel, 0)
    rel_f = np.maximum(rel, 1).astype(np.float32)
    log_ratio = np.log(rel_f / np.float32(16)) / np.float32(math.log(128 / 16))
    large = 16 + (log_ratio * np.float32(16)).astype(np.int32)
    large = np.minimum(large, 31)
    return np.where(rel < 16, rel, large)


def host_consts():
    s = np.arange(128)[:, None]
    t = np.arange(128)[None, :]
    oh = np.zeros((32, 2, 128, 128), np.float32)
    for which, off in ((0, 0), (1, 128)):
        rel = off + t - s
        b = _t5_bucket(rel)
        valid = rel >= 0
        for bb in range(32):
            oh[bb, which] = ((b == bb) & valid).astype(np.float32)
    caus01T = (s <= t).astype(np.float32)
    causneg = np.where(t <= s, 0.0, NEG).astype(np.float32)
    chunk = (s // 64) == (t // 64)
    m_strict = (chunk & (s > t)).astype(np.float32)
    m_inclT = (chunk & (s <= t)).astype(np.float32)
    negL = np.where(chunk & (s >= t), 0.0, NEG).astype(np.float32)
    negLT = np.where(chunk & (s <= t), 0.0, NEG).astype(np.float32)
    cumU = (chunk & (s <= t)).astype(np.float32)
    lastsel = (chunk & (s % 64 == 63) & True).astype(np.float32)
    lastU = np.zeros((128, 128), np.float32)
    lastU[:] = chunk.astype(np.float32)
    return {
        "c_ident": np.eye(128, dtype=np.float32),
        "c_oh": oh.reshape(32, 2 * 128 * 128),
        "c_caus01T": caus01T,
        "c_causneg": causneg,
        "c_mstrict": m_strict,
        "c_minclT": m_inclT,
        "c_negL": negL,
        "c_negLT": negLT,
        "c_cumU": cumU,
        "c_lastU": lastU,
    }


W_SHAPES = {
    "norm_w": [4, 1024], "a_w_in": [2, 1024, 1672], "a_g_cq": [2, 256],
    "a_w_uq": [2, 256, 8, 128], "a_w_uk": [2, 8, 128, 256], "a_g_q": [2, 256],
    "a_g_kv": [2, 256], "a_w_iq": [2, 256, 8, 128], "a_w_uv": [2, 8, 256, 128],
    "a_w_out": [2, 1024, 1024], "rel_bias": [32, 8], "b_w_in": [2, 1024, 4112],
    "b_conv_w": [2, 4, 3072], "b_a_log": [2, 8], "b_dt_bias": [2, 8], "b_g_o": [2, 128],
    "b_w_out": [2, 1024, 1024], "ple_norm": [4, 1024], "ple_w_gate": [4, 1024, 1024],
    "ple_w_proj": [4, 256, 1024],
}
C_SHAPES = {
    "c_ident": [128, 128], "c_oh": [32, 32768], "c_caus01T": [128, 128], "c_causneg": [128, 128],
    "c_mstrict": [128, 128], "c_minclT": [128, 128], "c_negL": [128, 128], "c_negLT": [128, 128],
    "c_cumU": [128, 128], "c_lastU": [128, 128],
}


def build_program(n_seq=2, layers=(0, 1, 2, 3), debug=None):
    T = n_seq * SEQ
    NT = T // 128
    nc = bass.Bass("TRN2", target_bir_lowering=False)
    dr = {}
    dr["x"] = nc.dram_tensor("x", [T, D], F32, kind="ExternalInput").ap()
    dr["p"] = nc.dram_tensor("p", [4, T, 256], F32, kind="ExternalInput").ap()
    for k, shp in W_SHAPES.items():
        dr[k] = nc.dram_tensor(k, shp, F32, kind="ExternalInput").ap()
    for k, shp in C_SHAPES.items():
        dr[k] = nc.dram_tensor(k, shp, F32, kind="ExternalInput").ap()
    y = nc.dram_tensor("y", [T, D], F32, kind="ExternalOutput").ap()
    xres = nc.dram_tensor("xres", [T, D], F32, kind="ExternalOutput").ap()
    zs = nc.dram_tensor("zs", [T, D], F32, kind="ExternalOutput").ap()
    cqnT_d = nc.dram_tensor("cqnT_d", [2, 128, T], BF16, kind=("ExternalOutput" if debug and "cqnT_d" in debug else "Internal")).ap()
    ckvT_d = nc.dram_tensor("ckvT_d", [2, 128, T], BF16, kind=("ExternalOutput" if debug and "ckvT_d" in debug else "Internal")).ap()
    kidxT_d = nc.dram_tensor("kidxT_d", [128, T], BF16, kind=("ExternalOutput" if debug and "kidxT_d" in debug else "Internal")).ap()
    ckva_d = nc.dram_tensor("ckva_d", [T, 257], BF16, kind=("ExternalOutput" if debug and "ckva_d" in debug else "Internal")).ap()
    widx_d = nc.dram_tensor("widx_d", [T, 8], F32, kind=("ExternalOutput" if debug and "widx_d" in debug else "Internal")).ap()
    bias_d = nc.dram_tensor("bias_d", [8, 2 * 128 * 128], F32, kind=("ExternalOutput" if debug and "bias_d" in debug else "Internal")).ap()
    qT_d = nc.dram_tensor("qT_d", [8, 128, T], F32, kind="ExternalOutput").ap()
    kT_d = nc.dram_tensor("kT_d", [8, 128, T], F32, kind="ExternalOutput").ap()
    vT_d = nc.dram_tensor("vT_d", [8, 128, T], F32, kind="ExternalOutput").ap()
    bg_d = nc.dram_tensor("bg_d", [T, 16], F32, kind="ExternalOutput").ap()

    P = Prog(nc)
    A = P.op
    dbg_tile = int(os.environ.get("DBG_TILE", "-1"))

    def dump(name, tile, key, shape, dt, i):
        if i != dbg_tile or not debug or name not in debug:
            return
        dd = nc.dram_tensor("dbg_" + name, shape, dt, kind="ExternalOutput").ap()
        P.store(dd, tile, key, dram=["dbg_" + name])

    def sb(name, shape, dt=F32):
        return nc.alloc_sbuf_tensor(name, shape, dt)

    identf = sb("identf", [128, 128])
    identb = sb("identb", [128, 128], BF16)
    epst = sb("epst", [128, 1])
    onesb = sb("onesb", [128, 128], BF16)
    onesf = sb("onesf", [128, 128])
    P.load(identf[:], dr["c_ident"], "identf")
    A(DVE, lambda e: e.tensor_copy(out=identb[:], in_=identf[:]), ["identf"], ["identb"])
    A(DVE, lambda e: e.memset(epst[:], EPS), [], ["epst"])
    A(DVE, lambda e: e.memset(onesb[:], 1.0), [], ["onesb"])
    A(DVE, lambda e: e.memset(onesf[:], 1.0), [], ["onesf"])

    pbank = [nc.alloc_psum_tensor("pb%d" % i, [128, 512], F32) for i in range(6)]
    tbank = [nc.alloc_psum_tensor("tb%d" % i, [128, 1024], BF16) for i in range(2)]

    class BankRing:
        def __init__(self, ids):
            self.ids = ids
            self.i = -1

        def next(self):
            self.i = (self.i + 1) % len(self.ids)
            b = self.ids[self.i]
            return pbank[b], ("pb", b)

    ringG = BankRing([0, 1, 2, 3])
    ringL = BankRing([4, 5])

    class TRing:
        def __init__(self):
            self.i = -1

        def next(self):
            self.i = (self.i + 1) % 2
            return tbank[self.i], ("tb", self.i)

    ringT = TRing()

    uid = [0]

    def mkring(es, name, shape, dt, bufs):
        uid[0] += 1
        tiles = [es.enter_context(nc.sbuf_tensor("%s_%d_u%d" % (name, i, uid[0]), shape, dt)) for i in range(bufs)]
        return Ring(tiles, name)

    def one(es, name, shape, dt=F32):
        uid[0] += 1
        return es.enter_context(nc.sbuf_tensor("%s_u%d" % (name, uid[0]), shape, dt))

    def rstd_from_ss(ss_ap, out_ap, inv_d, rk, wk):
        A(ACT, lambda e: e.activation(out=out_ap, in_=ss_ap, func=AF.Sqrt, scale=inv_d, bias=epst[:, 0:1]),
          [rk, "epst"], [wk])
        A(DVE, lambda e: e.reciprocal(out=out_ap, in_=out_ap), [wk], [wk])

    def load_w(stage, dst, src, n, gain=None, eng=DVE):
        st, k = stage.next()
        P.load(st[:, :n], src, k)
        if gain is not None:
            gap, gk = gain
            A(DVE, lambda e: e.tensor_scalar(out=dst, in0=st[:, :n], scalar1=gap, scalar2=None, op0=ALU.mult),
              [k, gk], ["wts"])
        elif eng == ACT:
            A(ACT, lambda e: e.copy(out=dst, in_=st[:, :n]), [k], ["wts"])
        else:
            A(DVE, lambda e: e.tensor_copy(out=dst, in_=st[:, :n]), [k], ["wts"])

    def alloc_post(es):
        W = {}
        W["wout"] = one(es, "wout", [128, 8, 1024], BF16)
        W["wg"] = one(es, "wg", [128, 8, 1024], BF16)
        W["wp"] = one(es, "wp", [128, 2, 1024], BF16)
        W["plg"] = one(es, "plg", [128, 8])
        W["zt"] = mkring(es, "zt", [128, 1024], F32, 1)
        W["xt2"] = mkring(es, "xt2", [128, 1024], F32, 1)
        W["oz"] = one(es, "oz", [128, 1024], BF16)
        W["ozT"] = one(es, "ozT", [128, 8, 128], BF16)
        W["x1"] = mkring(es, "x1", [128, 1024], F32, 2)
        W["x1b"] = one(es, "x1b", [128, 1024], BF16)
        W["x1T"] = one(es, "x1T", [128, 8, 128], BF16)
        W["gate"] = one(es, "gate", [128, 1024])
        W["pt"] = one(es, "pt", [128, 256])
        W["pb16"] = one(es, "pb16", [128, 256], BF16)
        W["pT"] = one(es, "pTs", [128, 2, 128], BF16)
        W["junk"] = one(es, "junkp", [128, 1024], BF16)
        W["ssp"] = one(es, "ssp", [128, 1])
        W["rsp"] = one(es, "rsp", [128, 1])
        return W

    def load_post_weights(W, stage, li, wout_src):
        P.load(W["plg"][:], dr["ple_norm"][li].rearrange("(c p) -> p c", p=128), "plg",
               allow_slow_non_contiguous=True)
        for kc in range(8):
            load_w(stage, W["wout"][:, kc, :], wout_src[kc * 128:(kc + 1) * 128, :], 1024, eng=ACT)
        for kc in range(8):
            load_w(stage, W["wg"][:, kc, :], dr["ple_w_gate"][li][kc * 128:(kc + 1) * 128, :], 1024,
                   gain=(W["plg"][:, kc:kc + 1], "plg"))
        for kc in range(2):
            load_w(stage, W["wp"][:, kc, :], dr["ple_w_proj"][li][kc * 128:(kc + 1) * 128, :], 1024, eng=ACT)

    def post_stage(W, li, i, o_aps, o_keys, xsrc, xdst, last, pp_ring=None):
        rows = slice(i * 128, (i + 1) * 128)
        zt, zk = W["zt"].next()
        P.load(zt[:], zs[rows, :], zk, dram=[("zs", i)])
        A(ACT, lambda e: e.activation(out=zt[:], in_=zt[:], func=AF.Silu), [zk], [zk])
        for hh in range(2):
            A(DVE, lambda e, hh=hh: e.tensor_tensor(out=W["oz"][:, hh * 512:(hh + 1) * 512], in0=o_aps[hh],
                                                    in1=zt[:, hh * 512:(hh + 1) * 512], op=ALU.mult),
              [o_keys[hh], zk], ["oz"])
        tb, tk = ringT.next()
        for kc in range(8):
            A(PE, lambda e, kc=kc: e.transpose(out=tb[:, kc * 128:(kc + 1) * 128],
                                               in_=W["oz"][:, kc * 128:(kc + 1) * 128], identity=identb[:]),
              ["oz", "identb"], [tk])
        A(ACT, lambda e: e.copy(out=W["ozT"][:].rearrange("p a b -> p (a b)"), in_=tb[:]), [tk], ["ozT"])
        ybanks = [ringG.next(), ringG.next()]
        for hh in range(2):
            pb_, pk = ybanks[hh]
            for kc in range(8):
                A(PE, lambda e, kc=kc, hh=hh, pb_=pb_: e.matmul(pb_[:], lhsT=W["ozT"][:, kc, :],
                                                              rhs=W["wout"][:, kc, hh * 512:(hh + 1) * 512],
                                                              start=(kc == 0), stop=(kc == 7)),
                  ["ozT", "wts"], [pk])
        xt, xk = W["xt2"].next()
        P.load(xt[:], xsrc[rows, :], xk, dram=[("x", i)])
        x1, x1k = W["x1"].next()
        for hh in range(2):
            pb_, pk = ybanks[hh]
            A(DVE, lambda e, hh=hh, pb_=pb_: e.tensor_tensor(out=x1[:, hh * 512:(hh + 1) * 512], in0=pb_[:],
                                                            in1=xt[:, hh * 512:(hh + 1) * 512], op=ALU.add),
              [pk, xk], [x1k])
        A(ACT, lambda e: e.activation(out=W["junk"][:], in_=x1[:], func=AF.Square, accum_out=W["ssp"][:]),
          [x1k], ["junkp", "ssp"])
        rstd_from_ss(W["ssp"][:], W["rsp"][:], 1.0 / D, "ssp", "rsp")
        A(DVE, lambda e: e.tensor_scalar(out=W["x1b"][:], in0=x1[:], scalar1=W["rsp"][:, 0:1], scalar2=None,
                                         op0=ALU.mult), [x1k, "rsp"], ["x1b"])
        tb, tk = ringT.next()
        for kc in range(8):
            A(PE, lambda e, kc=kc: e.transpose(out=tb[:, kc * 128:(kc + 1) * 128],
                                               in_=W["x1b"][:, kc * 128:(kc + 1) * 128], identity=identb[:]),
              ["x1b", "identb"], [tk])
        A(ACT, lambda e: e.copy(out=W["x1T"][:].rearrange("p a b -> p (a b)"), in_=tb[:]), [tk], ["x1T"])
        gbanks = [ringG.next(), ringG.next()]
        for hh in range(2):
            pb_, pk = gbanks[hh]
            for kc in range(8):
                A(PE, lambda e, kc=kc, hh=hh, pb_=pb_: e.matmul(pb_[:], lhsT=W["x1T"][:, kc, :],
                                                              rhs=W["wg"][:, kc, hh * 512:(hh + 1) * 512],
                                                              start=(kc == 0), stop=(kc == 7)),
                  ["x1T", "wts"], [pk])
            A(ACT, lambda e, hh=hh, pb_=pb_: e.activation(out=W["gate"][:, hh * 512:(hh + 1) * 512], in_=pb_[:],
                                                         func=AF.Sigmoid), [pk], ["gate"])
        P.load(W["pt"][:], dr["p"][li][rows, :], "pt")
        A(DVE, lambda e: e.tensor_copy(out=W["pb16"][:], in_=W["pt"][:]), ["pt"], ["pb16"])
        tb, tk = ringT.next()
        for kc in range(2):
            A(PE, lambda e, kc=kc: e.transpose(out=tb[:, kc * 128:(kc + 1) * 128],
                                               in_=W["pb16"][:, kc * 128:(kc + 1) * 128], identity=identb[:]),
              ["pb16", "identb"], [tk])
        A(ACT, lambda e: e.copy(out=W["pT"][:].rearrange("p a b -> p (a b)"), in_=tb[:, 0:256]), [tk], ["pT"])
        for hh in range(2):
            pb_, pk = (pp_ring or ringL).next()
            for kc in range(2):
                A(PE, lambda e, kc=kc, hh=hh, pb_=pb_: e.matmul(pb_[:], lhsT=W["pT"][:, kc, :],
                                                              rhs=W["wp"][:, kc, hh * 512:(hh + 1) * 512],
                                                              start=(kc == 0), stop=(kc == 1)),
                  ["pT", "wts"], [pk])
            A(DVE, lambda e, hh=hh, pb_=pb_: e.tensor_tensor(out=W["gate"][:, hh * 512:(hh + 1) * 512],
                                                            in0=W["gate"][:, hh * 512:(hh + 1) * 512],
                                                            in1=pb_[:], op=ALU.mult), [pk, "gate"], ["gate"])
        A(DVE, lambda e: e.tensor_tensor(out=x1[:], in0=x1[:], in1=W["gate"][:], op=ALU.add),
          [x1k, "gate"], [x1k])
        return P.store(xdst[rows, :], x1[:], x1k, dram=[("x", i)])

    def layer_A(li, j, xsrc, xdst, last):
        out_events = []
        with ExitStack() as es:
            stage = mkring(es, "stage", [128, 1024], F32, 2)
            win = one(es, "win", [128, 8, 1672], BF16)
            gn = one(es, "gn", [128, 8])
            gkvb = one(es, "gkvb", [128, 256])
            P.load(gn[:], dr["norm_w"][li].rearrange("(c p) -> p c", p=128), "gn", allow_slow_non_contiguous=True)
            P.load(gkvb[:], dr["a_g_kv"][j:j + 1, :].partition_broadcast(128), "gkvb")
            for kc in range(8):
                for c0 in (0, 836):
                    load_w(stage, win[:, kc, c0:c0 + 836], dr["a_w_in"][j][kc * 128:(kc + 1) * 128, c0:c0 + 836], 836,
                           gain=(gn[:, kc:kc + 1], "gn"))
            xtr = mkring(es, "xt", [128, 1024], F32, 2)
            junk = one(es, "junk", [128, 1024], BF16)
            ss = one(es, "ss", [128, 4])
            rs = one(es, "rs", [128, 4])
            xb = one(es, "xb", [128, 1024], BF16)
            xTr = mkring(es, "xT", [128, 8, 128], BF16, 2)
            ztr = mkring(es, "zt1", [128, 1024], F32, 2)
            nrm = mkring(es, "nrm", [128, 640], BF16, 2)
            ckvar = mkring(es, "ckva", [128, 257], BF16, 2)
            tps = mkring(es, "tps", [128, 5, 128], BF16, 2)
            widr = mkring(es, "wid", [128, 8], F32, 2)
            for r_ in ckvar.tiles:
                A(DVE, lambda e, r_=r_: e.memset(r_[:, 256:257], 1.0), [], [("ckva", ckvar.tiles.index(r_))])
            for i in range(NT):
                rows = slice(i * 128, (i + 1) * 128)
                xt, xk = xtr.next()
                P.load(xt[:], xsrc[rows, :], xk, dram=[("x", i)])
                A(ACT, lambda e, xt=xt: e.activation(out=junk[:], in_=xt[:], func=AF.Square, accum_out=ss[:, 0:1]),
                  [xk], ["junk", "ss"])
                rstd_from_ss(ss[:, 0:1], rs[:, 0:1], 1.0 / D, "ss", "rs")
                A(DVE, lambda e, xt=xt: e.tensor_scalar(out=xb[:], in0=xt[:], scalar1=rs[:, 0:1], scalar2=None,
                                                       op0=ALU.mult), [xk, "rs"], ["xb"])
                tb, tk = ringT.next()
                for kc in range(8):
                    A(PE, lambda e, kc=kc, tb=tb: e.transpose(out=tb[:, kc * 128:(kc + 1) * 128],
                                                             in_=xb[:, kc * 128:(kc + 1) * 128], identity=identb[:]),
                      ["xb", "identb"], [tk])
                xT, xTk = xTr.next()
                A(ACT, lambda e, xT=xT, tb=tb: e.copy(out=xT[:].rearrange("p a b -> p (a b)"), in_=tb[:]), [tk], [xTk])
                pA, pAk = ringG.next()
                pB, pBk = ringG.next()
                pZ0, pZ0k = ringG.next()
                pZ1, pZ1k = ringG.next()
                for (pb_, pk, c0, n) in ((pA, pAk, 0, 512), (pB, pBk, 512, 136), (pZ0, pZ0k, 648, 512),
                                         (pZ1, pZ1k, 1160, 512)):
                    for kc in range(8):
                        A(PE, lambda e, kc=kc, pb_=pb_, c0=c0, n=n, xT=xT: e.matmul(
                            pb_[:, 0:n], lhsT=xT[:, kc, :], rhs=win[:, kc, c0:c0 + n],
                            start=(kc == 0), stop=(kc == 7)), [xTk, "wts"], [pk])
                zt, zk = ztr.next()
                A(ACT, lambda e, zt=zt, pZ0=pZ0: e.copy(out=zt[:, 0:512], in_=pZ0[:]), [pZ0k], [zk])
                A(ACT, lambda e, zt=zt, pZ1=pZ1: e.copy(out=zt[:, 512:1024], in_=pZ1[:]), [pZ1k], [zk])
                P.store(zs[rows, :], zt[:], zk, dram=[("zs", i)])
                A(ACT, lambda e, pA=pA: e.activation(out=junk[:, 0:256], in_=pA[:, 0:256], func=AF.Square,
                                                     accum_out=ss[:, 1:2]), [pAk], ["junk", "ss"])
                A(ACT, lambda e, pA=pA: e.activation(out=junk[:, 256:512], in_=pA[:, 256:512], func=AF.Square,
                                                     accum_out=ss[:, 2:3]), [pAk], ["junk", "ss"])
                A(ACT, lambda e, pB=pB: e.activation(out=junk[:, 512:640], in_=pB[:, 0:128], func=AF.Square,
                                                     accum_out=ss[:, 3:4]), [pBk], ["junk", "ss"])
                A(ACT, lambda e: e.activation(out=rs[:, 1:3], in_=ss[:, 1:3], func=AF.Sqrt, scale=1.0 / 256,
                                              bias=epst[:, 0:1]), ["ss", "epst"], ["rs"])
                A(ACT, lambda e: e.activation(out=rs[:, 3:4], in_=ss[:, 3:4], func=AF.Sqrt, scale=1.0 / 128,
                                              bias=epst[:, 0:1]), ["ss", "epst"], ["rs"])
                A(DVE, lambda e: e.reciprocal(out=rs[:, 1:4], in_=rs[:, 1:4]), ["rs"], ["rs"])
                nr, nk = nrm.next()
                ca, cak = ckvar.next()
                A(DVE, lambda e, nr=nr, pA=pA: e.tensor_scalar(out=nr[:, 0:256], in0=pA[:, 0:256],
                                                             scalar1=rs[:, 1:2], scalar2=None, op0=ALU.mult),
                  [pAk, "rs"], [nk])
                A(DVE, lambda e, ca=ca, pA=pA: e.scalar_tensor_tensor(out=ca[:, 0:256], in0=pA[:, 256:512],
                                                                    scalar=rs[:, 2:3], in1=gkvb[:],
                                                                    op0=ALU.mult, op1=ALU.mult),
                  [pAk, "rs", "gkvb"], [cak])
                A(DVE, lambda e, nr=nr, pB=pB: e.tensor_scalar(out=nr[:, 512:640], in0=pB[:, 0:128],
                                                             scalar1=rs[:, 3:4], scalar2=None, op0=ALU.mult),
                  [pBk, "rs"], [nk])
                wd, wdk = widr.next()
                A(DVE, lambda e, wd=wd, pB=pB: e.tensor_scalar(out=wd[:], in0=pB[:, 128:136], scalar1=8 ** -0.5,
                                                             scalar2=None, op0=ALU.mult), [pBk], [wdk])
                P.store(widx_d[rows, :], wd[:], wdk, dram=[("widx", i)])
                P.store(ckva_d[rows, :], ca[:], cak, dram=[("ckva", i)])
                tb, tk = ringT.next()
                srcs = [nr[:, 0:128], nr[:, 128:256], ca[:, 0:128], ca[:, 128:256], nr[:, 512:640]]
                for q_, s_ in enumerate(srcs):
                    A(PE, lambda e, q_=q_, s_=s_, tb=tb: e.transpose(out=tb[:, q_ * 128:(q_ + 1) * 128], in_=s_,
                                                                    identity=identb[:]),
                      [nk, cak, "identb"], [tk])
                tp, tpk = tps.next()
                A(ACT, lambda e, tp=tp, tb=tb: e.copy(out=tp[:].rearrange("p a b -> p (a b)"), in_=tb[:, 0:640]),
                  [tk], [tpk])
                P.store(cqnT_d[:, :, rows].rearrange("k p t -> p k t"), tp[:, 0:2, :], tpk, dram=[("cqnT", i)])
                P.store(ckvT_d[:, :, rows].rearrange("k p t -> p k t"), tp[:, 2:4, :], tpk, dram=[("ckvT", i)])
                P.store(kidxT_d[:, rows], tp[:, 4, :], tpk, dram=[("kidxT", i)])
                if "a1" in os.environ.get("DBG_BARRIER", ""):
                    P.barrier()
        P.barrier()

        with ExitStack() as es:
            stage = mkring(es, "stage", [128, 1024], F32, 2)
            W = alloc_post(es)
            load_post_weights(W, stage, li, dr["a_w_out"][j])
            gcq = one(es, "gcq", [128, 2])
            gq = one(es, "gq", [128, 2])
            P.load(gcq[:], dr["a_g_cq"][j].rearrange("(c p) -> p c", p=128), "gcq", allow_slow_non_contiguous=True)
            P.load(gq[:], dr["a_g_q"][j].rearrange("(c p) -> p c", p=128), "gq", allow_slow_non_contiguous=True)
            wuq = one(es, "wuq", [128, 2, 1024], BF16)
            wiq = one(es, "wiq", [128, 2, 1024], BF16)
            wuk = one(es, "wuk", [128, 8, 256], BF16)
            wuv = one(es, "wuv", [128, 8, 2, 128], BF16)
            for kc in range(2):
                load_w(stage, wuq[:, kc, :], dr["a_w_uq"][j][kc * 128:(kc + 1) * 128].rearrange("p h d -> p (h d)"),
                       1024, gain=(gcq[:, kc:kc + 1], "gcq"))
                load_w(stage, wiq[:, kc, :], dr["a_w_iq"][j][kc * 128:(kc + 1) * 128].rearrange("p h d -> p (h d)"),
                       1024, gain=(gcq[:, kc:kc + 1], "gcq"))
            for h0 in (0, 4):
                st, k = stage.next()
                P.load(st[:].rearrange("p (h c) -> p h c", h=4), dr["a_w_uk"][j][h0:h0 + 4].rearrange("h d c -> d h c"), k)
                A(ACT, lambda e, st=st, h0=h0: e.copy(out=wuk[:, h0:h0 + 4, :].rearrange("p h c -> p (h c)"), in_=st[:]),
                  [k], ["wts"])
                st, k = stage.next()
                for hh in range(4):
                    P.load(st[:, hh * 256:(hh + 1) * 256].rearrange("p (k v) -> p k v", k=2),
                           dr["a_w_uv"][j][h0 + hh].rearrange("(k p) v -> p k v", p=128), k)
                A(ACT, lambda e, st=st, h0=h0: e.copy(out=wuv[:, h0:h0 + 4, :, :].rearrange("p h k v -> p (h k v)"),
                                                     in_=st[:]), [k], ["wts"])
            BT = one(es, "BT", [128, 2, 8, 128])
            for w_ in range(2):
                P.load(BT[:, w_, :, :], bias_d.rearrange("h (w s t) -> w s h t", w=2, s=128)[w_], "BT", dram=["bias_d"])
            caus01T = one(es, "caus01T", [128, 128], BF16)
            causneg = one(es, "causneg", [128, 128])
            st, k = stage.next()
            P.load(st[:, 0:128], dr["c_caus01T"], k)
            A(DVE, lambda e, st=st: e.tensor_copy(out=caus01T[:], in_=st[:, 0:128]), [k], ["caus01T"])
            P.load(causneg[:], dr["c_causneg"], "causneg")
            cqnT = one(es, "cqnT", [128, 2, SEQ], BF16)
            ckvT = one(es, "ckvT", [128, 2, SEQ], BF16)
            kidxT = one(es, "kidxT", [128, SEQ], BF16)
            ckva = one(es, "ckva_r", [128, 16, 257], BF16)
            widx = one(es, "widx", [128, 16, 8])
            qnT = one(es, "qnT", [128, 8, 128], BF16)
            qtok = one(es, "qtok", [128, 8, 256], BF16)
            qhT = one(es, "qhT", [128, 2, 8, 128], BF16)
            qiT = one(es, "qiT", [128, 8, 128], BF16)
            ssq = one(es, "ssq", [128, 8])
            qsc = one(es, "qsc", [128, 8])
            junkq = one(es, "junkq", [128, 256], BF16)
            score = one(es, "score", [128, SEQ])
            work = one(es, "work", [128, SEQ])
            mx = one(es, "mx", [128, 8])
            mask01 = one(es, "mask01", [128, SEQ], BF16)
            maskT = one(es, "maskT", [128, 16, 128], BF16)
            relur = mkring(es, "relu", [128, 512], BF16, 3)
            Er = mkring(es, "E", [128, 4, 128], BF16, 3)
            tmpr = mkring(es, "tmpL", [128, 512], F32, 2)
            rden = one(es, "rden", [128, 8])
            olat = one(es, "olat", [128, 8, 256], BF16)
            olT = one(es, "olT", [128, 8, 2, 128], BF16)

            for s in range(n_seq):
                base = s * SEQ
                tl = list(range(s * 16, (s + 1) * 16))
                P.load(cqnT[:], cqnT_d[:, :, base:base + SEQ].rearrange("k p t -> p k t"), "cqnT",
                       dram=[("cqnT", i) for i in tl])
                P.load(ckvT[:], ckvT_d[:, :, base:base + SEQ].rearrange("k p t -> p k t"), "ckvT",
                       dram=[("ckvT", i) for i in tl])
                P.load(kidxT[:], kidxT_d[:, base:base + SEQ], "kidxT", dram=[("kidxT", i) for i in tl])
                P.load(ckva[:], ckva_d[base:base + SEQ, :].rearrange("(n p) c -> p n c", p=128), "ckva_r",
                       dram=[("ckva", i) for i in tl])
                P.load(widx[:], widx_d[base:base + SEQ, :].rearrange("(n p) c -> p n c", p=128), "widx",
                       dram=[("widx", i) for i in tl])
                for qt in range(16):
                    i = s * 16 + qt
                    tcols = slice(qt * 128, (qt + 1) * 128)
                    Sc = (qt + 1) * 128
                    for hg in range(2):
                        pb_, pk = ringG.next()
                        for hh in range(4):
                            h = hg * 4 + hh
                            for kc in range(2):
                                A(PE, lambda e, pb_=pb_, hh=hh, h=h, kc=kc: e.matmul(
                                    pb_[:, hh * 128:(hh + 1) * 128], lhsT=wuq[:, kc, h * 128:(h + 1) * 128],
                                    rhs=cqnT[:, kc, tcols], start=(kc == 0), stop=(kc == 1)),
                                  ["wts", "cqnT"], [pk])
                        A(ACT, lambda e, pb_=pb_, hg=hg: e.copy(
                            out=qnT[:, hg * 4:(hg + 1) * 4, :].rearrange("p a b -> p (a b)"), in_=pb_[:]),
                          [pk], ["qnT"])
                    for hg in range(2):
                        pb_, pk = ringG.next()
                        for hh in range(4):
                            h = hg * 4 + hh
                            for kc in range(2):
                                A(PE, lambda e, pb_=pb_, hh=hh, h=h, kc=kc: e.matmul(
                                    pb_[:, hh * 128:(hh + 1) * 128], lhsT=wiq[:, kc, h * 128:(h + 1) * 128],
                                    rhs=cqnT[:, kc, tcols], start=(kc == 0), stop=(kc == 1)),
                                  ["wts", "cqnT"], [pk])
                        A(ACT, lambda e, pb_=pb_, hg=hg: e.mul(
                            out=qiT[:, hg * 4:(hg + 1) * 4, :].rearrange("p a b -> p (a b)"), in_=pb_[:],
                            mul=128 ** -0.5), [pk], ["qiT"])
                    for h2 in range(4):
                        pb_, pk = ringG.next()
                        for hh in range(2):
                            h = h2 * 2 + hh
                            A(PE, lambda e, pb_=pb_, hh=hh, h=h: e.matmul(
                                pb_[:, hh * 256:(hh + 1) * 256], lhsT=qnT[:, h, :], rhs=wuk[:, h, :],
                                start=True, stop=True), ["qnT", "wts"], [pk])
                            A(ACT, lambda e, pb_=pb_, hh=hh, h=h: e.activation(
                                out=junkq[:], in_=pb_[:, hh * 256:(hh + 1) * 256], func=AF.Square,
                                accum_out=ssq[:, h:h + 1]), [pk], ["junkq", "ssq"])
                        c2 = slice(h2 * 2, h2 * 2 + 2)
                        A(ACT, lambda e, c2=c2: e.activation(out=qsc[:, c2], in_=ssq[:, c2], func=AF.Sqrt,
                                                             scale=1.0 / 256, bias=epst[:, 0:1]),
                          ["ssq", "epst"], ["qsc"])
                        A(DVE, lambda e, c2=c2: e.reciprocal(out=qsc[:, c2], in_=qsc[:, c2]), ["qsc"], ["qsc"])
                        for hh in range(2):
                            h = h2 * 2 + hh
                            A(DVE, lambda e, pb_=pb_, hh=hh, h=h: e.tensor_scalar(
                                out=qtok[:, h, :], in0=pb_[:, hh * 256:(hh + 1) * 256], scalar1=qsc[:, h:h + 1],
                                scalar2=1.0 / 16, op0=ALU.mult, op1=ALU.mult), [pk, "qsc"], ["qtok"])
                    for kc in range(2):
                        tb, tk = ringT.next()
                        for h in range(8):
                            A(PE, lambda e, tb=tb, h=h, kc=kc: e.transpose(
                                out=tb[:, h * 128:(h + 1) * 128], in_=qtok[:, h, kc * 128:(kc + 1) * 128],
                                identity=identb[:]), ["qtok", "identb"], [tk])
                        A(DVE, lambda e, tb=tb, kc=kc: e.tensor_scalar(
                            out=qhT[:, kc, :, :].rearrange("p a b -> p (a b)"), in0=tb[:], scalar1=gq[:, kc:kc + 1],
                            scalar2=None, op0=ALU.mult), [tk, "gq"], ["qhT"])
                    dump("qhT", qhT[:], "qhT", [128, 2, 8, 128], BF16, i)
                    dump("qtok", qtok[:], "qtok", [128, 8, 256], BF16, i)
                    dump("qnT", qnT[:], "qnT", [128, 8, 128], BF16, i)
                    if qt >= 2:
                        for h in range(8):
                            for c0 in range(0, Sc, 512):
                                n = min(512, Sc - c0)
                                pb_, pk = ringL.next()
                                A(PE, lambda e, pb_=pb_, h=h, c0=c0, n=n: e.matmul(
                                    pb_[:, 0:n], lhsT=qiT[:, h, :], rhs=kidxT[:, c0:c0 + n], start=True, stop=True),
                                  ["qiT", "kidxT"], [pk])
                                rl, rk = relur.next()
                                A(ACT, lambda e, pb_=pb_, rl=rl, n=n: e.activation(out=rl[:, 0:n], in_=pb_[:, 0:n],
                                                                                 func=AF.Relu), [pk], [rk])
                                if h == 0:
                                    A(DVE, lambda e, rl=rl, c0=c0, n=n: e.tensor_scalar(
                                        out=score[:, c0:c0 + n], in0=rl[:, 0:n], scalar1=widx[:, qt, 0:1],
                                        scalar2=None, op0=ALU.mult), [rk, "widx"], ["score"])
                                else:
                                    A(DVE, lambda e, rl=rl, c0=c0, n=n, h=h: e.scalar_tensor_tensor(
                                        out=score[:, c0:c0 + n], in0=rl[:, 0:n], scalar=widx[:, qt, h:h + 1],
                                        in1=score[:, c0:c0 + n], op0=ALU.mult, op1=ALU.add),
                                      [rk, "widx", "score"], ["score"])
                        A(DVE, lambda e: e.tensor_tensor(out=score[:, tcols], in0=score[:, tcols], in1=causneg[:],
                                                         op=ALU.add), ["score", "causneg"], ["score"])
                        for r in range(32):
                            src_ = score if r == 0 else work
                            A(DVE, lambda e, src_=src_: e.max(out=mx[:], in_=src_[:, 0:Sc]),
                              ["score", "work"], ["mx"])
                            if r < 31:
                                A(DVE, lambda e, src_=src_: e.match_replace(out=work[:, 0:Sc], in_to_replace=mx[:],
                                                                           in_values=src_[:, 0:Sc], imm_value=NEG),
                                  ["score", "work", "mx"], ["work"])
                        A(DVE, lambda e: e.tensor_scalar(out=mask01[:, 0:Sc], in0=score[:, 0:Sc], scalar1=mx[:, 7:8],
                                                         scalar2=None, op0=ALU.is_ge), ["score", "mx"], ["mask01"])
                        for g0 in range(0, qt + 1, 8):
                            g1 = min(qt + 1, g0 + 8)
                            tb, tk = ringT.next()
                            for kt in range(g0, g1):
                                A(PE, lambda e, tb=tb, kt=kt, g0=g0: e.transpose(
                                    out=tb[:, (kt - g0) * 128:(kt - g0 + 1) * 128],
                                    in_=mask01[:, kt * 128:(kt + 1) * 128], identity=identb[:]),
                                  ["mask01", "identb"], [tk])
                            A(ACT, lambda e, tb=tb, g0=g0, g1=g1: e.copy(
                                out=maskT[:, g0:g1, :].rearrange("p a b -> p (a b)"), in_=tb[:, 0:(g1 - g0) * 128]),
                              [tk], ["maskT"])
                    else:
                        for kt in range(qt + 1):
                            if kt == qt:
                                A(DVE, lambda e, kt=kt: e.tensor_copy(out=maskT[:, kt, :], in_=caus01T[:]),
                                  ["caus01T"], ["maskT"])
                            else:
                                A(DVE, lambda e, kt=kt: e.memset(maskT[:, kt, :], 1.0), [], ["maskT"])
                    for hg in range(2):
                        for kt in range(qt + 1):
                            kcols = slice(kt * 128, (kt + 1) * 128)
                            pL, pLk = ringL.next()
                            for kc in range(2):
                                A(PE, lambda e, pL=pL, kc=kc, kcols=kcols, hg=hg: e.matmul(
                                    pL[:], lhsT=ckvT[:, kc, kcols],
                                    rhs=qhT[:, kc, hg * 4:(hg + 1) * 4, :].rearrange("p a b -> p (a b)"),
                                    start=(kc == 0), stop=(kc == 1)), ["ckvT", "qhT"], [pLk])
                            E, Ek = Er.next()
                            Ef = E[:].rearrange("p a b -> p (a b)")
                            if kt >= qt - 1:
                                w_ = 0 if kt == qt else 1
                                tm, tmk = tmpr.next()
                                A(DVE, lambda e, tm=tm, pL=pL, w_=w_, hg=hg: e.tensor_tensor(
                                    out=tm[:], in0=pL[:],
                                    in1=BT[:, w_, hg * 4:(hg + 1) * 4, :].rearrange("p a b -> p (a b)"), op=ALU.add),
                                  [pLk, "BT"], [tmk])
                                A(ACT, lambda e, tm=tm, Ef=Ef: e.activation(out=Ef, in_=tm[:], func=AF.Exp),
                                  [tmk], [(Ek, q) for q in range(4)])
                            else:
                                A(ACT, lambda e, pL=pL, Ef=Ef: e.activation(out=Ef, in_=pL[:], func=AF.Exp),
                                  [pLk], [(Ek, q) for q in range(4)])
                            for hh in range(4):
                                A(POOL if hh == 3 else DVE, lambda e, E=E, hh=hh, kt=kt: e.tensor_tensor(
                                    out=E[:, hh, :], in0=E[:, hh, :], in1=maskT[:, kt, :], op=ALU.mult),
                                  [(Ek, hh), "maskT"], [(Ek, hh)])
                            for hh in range(4):
                                A(PE, lambda e, E=E, hh=hh, kt=kt: e.matmul(
                                    pbank[hh][:, 0:257], lhsT=E[:, hh, :], rhs=ckva[:, kt, :],
                                    start=(kt == 0), stop=(kt == qt)), [(Ek, hh), "ckva_r"], [("pb", hh)])
                        for hh in range(4):
                            h = hg * 4 + hh
                            A(DVE, lambda e, hh=hh, h=h: e.reciprocal(out=rden[:, h:h + 1], in_=pbank[hh][:, 256:257]),
                              [("pb", hh)], ["rden"])
                            A(DVE, lambda e, hh=hh, h=h: e.tensor_scalar(
                                out=olat[:, h, :], in0=pbank[hh][:, 0:256], scalar1=rden[:, h:h + 1], scalar2=None,
                                op0=ALU.mult), [("pb", hh), "rden"], ["olat"])
                    dump("olat", olat[:], "olat", [128, 8, 256], BF16, i)
                    dump("maskT", maskT[:], "maskT", [128, 16, 128], BF16, i)
                    dump("score", score[:], "score", [128, SEQ], F32, i)
                    dump("mx", mx[:], "mx", [128, 8], F32, i)
                    for kc in range(2):
                        tb, tk = ringT.next()
                        for h in range(8):
                            A(PE, lambda e, tb=tb, h=h, kc=kc: e.transpose(
                                out=tb[:, h * 128:(h + 1) * 128], in_=olat[:, h, kc * 128:(kc + 1) * 128],
                                identity=identb[:]), ["olat", "identb"], [tk])
                        A(ACT, lambda e, tb=tb, kc=kc: e.copy(out=olT[:, :, kc, :],
                                                              in_=tb[:].rearrange("p (a b) -> p a b", a=8)),
                          [tk], ["olT"])
                    obanks = [ringL.next(), ringL.next()]
                    for h in range(8):
                        pb_, pk = obanks[h // 4]
                        for kc in range(2):
                            A(PE, lambda e, pb_=pb_, h=h, kc=kc: e.matmul(
                                pb_[:, (h % 4) * 128:(h % 4 + 1) * 128], lhsT=olT[:, h, kc, :], rhs=wuv[:, h, kc, :],
                                start=(kc == 0), stop=(kc == 1)), ["olT", "wts"], [pk])
                    ev = post_stage(W, li, i, [obanks[0][0][:], obanks[1][0][:]], [obanks[0][1], obanks[1][1]],
                                    xsrc, xdst, last)
                    out_events.append(ev)
                    if "a3" in os.environ.get("DBG_BARRIER", ""):
                        P.barrier()
        P.barrier()
        return out_events

    def layer_B(li, j, xsrc, xdst, last):
        out_events = []
        NB = T // 512
        with ExitStack() as es:
            stage = mkring(es, "stage", [128, 1024], F32, 2)
            win = one(es, "winb", [128, 8, 4112], BF16)
            gn = one(es, "gn", [128, 8])
            P.load(gn[:], dr["norm_w"][li].rearrange("(c p) -> p c", p=128), "gn", allow_slow_non_contiguous=True)
            for kc in range(8):
                for c0 in range(0, 4112, 514):
                    load_w(stage, win[:, kc, c0:c0 + 514], dr["b_w_in"][j][kc * 128:(kc + 1) * 128, c0:c0 + 514],
                           514, gain=(gn[:, kc:kc + 1], "gn"))
            cwr = one(es, "cwr", [96, 128])
            cw = one(es, "cw", [128, 4, 24])
            P.load(cwr[:], dr["b_conv_w"][j].rearrange("w (c p) -> (w c) p", p=128), "cwr")
            pq, pqk = ringL.next()
            A(PE, lambda e: e.transpose(out=pq[:, 0:96], in_=cwr[:], identity=identf[0:96, 0:96]),
              ["cwr", "identf"], [pqk])
            A(ACT, lambda e: e.copy(out=cw[:].rearrange("p w c -> p (w c)"), in_=pq[:, 0:96]), [pqk], ["cw"])
            dtb = one(es, "dtb", [128, 8])
            nea = one(es, "nea", [128, 8])
            one1 = one(es, "one1", [128, 1])
            A(DVE, lambda e: e.memset(one1[:], 1.0), [], ["one1"])
            P.load(dtb[:], dr["b_dt_bias"][j:j + 1, :].partition_broadcast(128), "dtb")
            P.load(nea[:], dr["b_a_log"][j:j + 1, :].partition_broadcast(128), "nea")
            A(ACT, lambda e: e.activation(out=nea[:], in_=nea[:], func=AF.Exp), ["nea"], ["nea"])
            A(DVE, lambda e: e.tensor_scalar(out=nea[:], in0=nea[:], scalar1=-1.0, scalar2=None, op0=ALU.mult),
              ["nea"], ["nea"])
            xtr = mkring(es, "xt", [128, 1024], F32, 2)
            junk = one(es, "junk", [128, 1024], BF16)
            ss = one(es, "ss", [128, 1])
            rs = one(es, "rs", [128, 1])
            xb = one(es, "xb", [128, 1024], BF16)
            xT = one(es, "xTb", [128, 8, 512], BF16)
            ztr = mkring(es, "zt1", [128, 1024], F32, 2)
            bgr = mkring(es, "bgt", [128, 16], F32, 2)
            hal = one(es, "hal", [128, 24, 3])
            rawr = mkring(es, "raw", [128, 515], F32, 2)
            accr = mkring(es, "acc", [128, 512], F32, 2)
            sqr = mkring(es, "sqr", [128, 512], BF16, 2)
            rnr = mkring(es, "rnr", [128, 512], F32, 2)
            outr = mkring(es, "outr", [128, 512], F32, 3)
            dbg_b1 = int(os.environ.get("DBG_B1", "9"))
            for b in range(NB if dbg_b1 >= 2 else 0):
                seq_start = (b % 4 == 0)
                bcols = slice(b * 512, (b + 1) * 512)
                for tt in range(4):
                    i = b * 4 + tt
                    rows = slice(i * 128, (i + 1) * 128)
                    xt, xk = xtr.next()
                    P.load(xt[:], xsrc[rows, :], xk, dram=[("x", i)])
                    A(ACT, lambda e: e.activation(out=junk[:], in_=xt[:], func=AF.Square, accum_out=ss[:, 0:1]),
                      [xk], ["junk", "ss"])
                    rstd_from_ss(ss[:, 0:1], rs[:, 0:1], 1.0 / D, "ss", "rs")
                    A(DVE, lambda e: e.tensor_scalar(out=xb[:], in0=xt[:], scalar1=rs[:, 0:1], scalar2=None,
                                                     op0=ALU.mult), [xk, "rs"], ["xb"])
                    tb, tk = ringT.next()
                    for kc in range(8):
                        A(PE, lambda e: e.transpose(out=tb[:, kc * 128:(kc + 1) * 128],
                                                    in_=xb[:, kc * 128:(kc + 1) * 128], identity=identb[:]),
                          ["xb", "identb"], [tk])
                    A(ACT, lambda e: e.copy(out=xT[:, :, tt * 128:(tt + 1) * 128],
                                            in_=tb[:].rearrange("p (a b) -> p a b", a=8)), [tk], [("xTb", tt)])
                    pZ0, pZ0k = ringG.next()
                    pZ1, pZ1k = ringG.next()
                    pB, pBk = ringG.next()
                    for (pb_, pk, c0, n) in ((pZ0, pZ0k, 3088, 512), (pZ1, pZ1k, 3600, 512), (pB, pBk, 3072, 16)):
                        for kc in range(8):
                            A(PE, lambda e: e.matmul(pb_[:, 0:n], lhsT=xT[:, kc, tt * 128:(tt + 1) * 128],
                                                     rhs=win[:, kc, c0:c0 + n], start=(kc == 0), stop=(kc == 7)),
                              [("xTb", tt), "wts"], [pk])
                    zt, zk = ztr.next()
                    A(ACT, lambda e: e.copy(out=zt[:, 0:512], in_=pZ0[:]), [pZ0k], [zk])
                    A(ACT, lambda e: e.copy(out=zt[:, 512:1024], in_=pZ1[:]), [pZ1k], [zk])
                    P.store(zs[rows, :], zt[:], zk, dram=[("zs", i)])
                    bg, bgk = bgr.next()
                    A(ACT, lambda e: e.activation(out=bg[:, 0:8], in_=pB[:, 0:8], func=AF.Sigmoid), [pBk], [bgk])
                    A(DVE, lambda e: e.tensor_tensor(out=bg[:, 8:16], in0=pB[:, 8:16], in1=dtb[:], op=ALU.add),
                      [pBk, "dtb"], [bgk])
                    A(ACT, lambda e: e.activation(out=bg[:, 8:16], in_=bg[:, 8:16], func=AF.Exp), [bgk], [bgk])
                    A(ACT, lambda e: e.activation(out=bg[:, 8:16], in_=bg[:, 8:16], func=AF.Ln, bias=one1[:, 0:1]),
                      [bgk, "one1"], [bgk])
                    A(DVE, lambda e: e.tensor_tensor(out=bg[:, 8:16], in0=bg[:, 8:16], in1=nea[:], op=ALU.mult),
                      [bgk, "nea"], [bgk])
                    P.store(bg_d[rows, :], bg[:], bgk, dram=[("bg", i)])
                xkeys = [("xTb", q) for q in range(4)]
                for c in range(24 if dbg_b1 >= 3 else 0):
                    pb_, pk = ringL.next()
                    for kc in range(8):
                        A(PE, lambda e: e.matmul(pb_[:], lhsT=win[:, kc, c * 128:(c + 1) * 128], rhs=xT[:, kc, :],
                                                 start=(kc == 0), stop=(kc == 7)), xkeys + ["wts"], [pk])
                    raw, rk = rawr.next()
                    if seq_start:
                        A(DVE, lambda e: e.memset(raw[:, 0:3], 0.0), [], [rk])
                    else:
                        A(DVE, lambda e: e.tensor_copy(out=raw[:, 0:3], in_=hal[:, c, :]), [("hal", c)], [rk])
                    A(ACT, lambda e: e.copy(out=raw[:, 3:515], in_=pb_[:]), [pk], [rk])
                    A(DVE, lambda e: e.tensor_copy(out=hal[:, c, :], in_=raw[:, 512:515]), [rk], [("hal", c)])
                    dbg_c = int(os.environ.get("DBG_B1C", "9"))
                    if dbg_c < 2:
                        continue
                    acc, ak = accr.next()
                    A(DVE, lambda e: e.tensor_scalar(out=acc[:], in0=raw[:, 3:515], scalar1=cw[:, 3, c:c + 1],
                                                     scalar2=None, op0=ALU.mult), [rk, "cw"], [ak])
                    for w_ in (2, 1, 0):
                        eng_ = DVE
                        A(eng_, lambda e: e.scalar_tensor_tensor(out=acc[:], in0=raw[:, w_:w_ + 512],
                                                                 scalar=cw[:, w_, c:c + 1], in1=acc[:],
                                                                 op0=ALU.mult, op1=ALU.add), [rk, "cw", ak], [ak])
                    A(ACT, lambda e: e.activation(out=acc[:], in_=acc[:], func=AF.Silu), [ak], [ak])
                    h = c % 8
                    if dbg_c < 3:
                        continue
                    if c < 16:
                        sq, sk = sqr.next()
                        A(ACT, lambda e: e.activation(out=sq[:], in_=acc[:], func=AF.Square), [ak], [sk])
                        pn, pnk = ringG.next()
                        A(PE, lambda e: e.matmul(pn[:], lhsT=onesb[:], rhs=sq[:], start=True, stop=True),
                          [sk, "onesb"], [pnk])
                        rn, rnk = rnr.next()
                        A(ACT, lambda e: e.activation(out=rn[:], in_=pn[:], func=AF.Sqrt, bias=epst[:, 0:1]),
                          [pnk, "epst"], [rnk])
                        A(DVE, lambda e: e.reciprocal(out=rn[:], in_=rn[:]), [rnk], [rnk])
                        ot, ok_ = outr.next()
                        if c < 8:
                            A(DVE, lambda e: e.scalar_tensor_tensor(out=ot[:], in0=acc[:], scalar=128 ** -0.5,
                                                                    in1=rn[:], op0=ALU.mult, op1=ALU.mult),
                              [ak, rnk], [ok_])
                            P.store(qT_d[h][:, bcols], ot[:], ok_, dram=[("qT", b)])
                        else:
                            A(DVE, lambda e: e.tensor_tensor(out=ot[:], in0=acc[:], in1=rn[:], op=ALU.mult),
                              [ak, rnk], [ok_])
                            P.store(kT_d[h][:, bcols], ot[:], ok_, dram=[("kT", b)])
                    else:
                        P.store(vT_d[h][:, bcols], acc[:], ak, dram=[("vT", b)])
        P.barrier()

        if os.environ.get("DBG_B") == "1":
            return out_events
        with ExitStack() as es:
            stage = mkring(es, "stage", [128, 1024], F32, 2)
            W = alloc_post(es)
            load_post_weights(W, stage, li, dr["b_w_out"][j])
            gob = one(es, "gob", [128, 128])
            P.load(gob[:], dr["b_g_o"][j:j + 1, :].partition_broadcast(128), "gob")
            cst = {}
            for nm in ("c_mstrict", "c_negL", "c_negLT", "c_cumU", "c_lastU"):
                cst[nm] = one(es, nm, [128, 128])
                P.load(cst[nm][:], dr[nm], nm)
            mstrict, negL, negLT, cumU, lastU = (cst[n] for n in ("c_mstrict", "c_negL", "c_negLT", "c_cumU", "c_lastU"))
            qTb = one(es, "qTb", [128, 8, 128])
            kTb = one(es, "kTb", [128, 8, 128])
            vTb = one(es, "vTb", [128, 8, 128])
            bgb = one(es, "bgb", [128, 16])
            sm = one(es, "sm", [128, 8, 8])
            S = one(es, "S", [128, 8, 128])
            og = one(es, "og", [128, 8, 128])
            on = one(es, "on", [128, 1024])
            ss8 = one(es, "ss8", [128, 8])
            rs8 = one(es, "rs8", [128, 8])
            junkb = one(es, "junkb", [128, 128], BF16)
            names = ["ktok", "vtok", "Gb", "tmp", "Dm", "DT", "Pm", "PT", "P2", "P2T", "TT", "vb", "kbg", "upre",
                     "wdT", "aqkT", "EG", "qdT", "kdec", "u", "kT2", "GCr"]
            H = {nm: one(es, "h_" + nm, [128, 8, 128]) for nm in names}
            cdb = one(es, "cdb", [128, 8, 2])
            phase = [0]

            def new_stage():
                phase[0] ^= 1

            def slot(h):
                bank = (2 if phase[0] else 4) + h // 4
                off = (h % 4) * 128
                return pbank[bank][:, off:off + 128], ("pb", bank), pbank[bank], off

            spl = {nm: mkring(es, "spl_" + nm, [128, 128], BF16, 4) for nm in ("lh", "ll", "rh", "rl")}

            def mm32(out, lhsT, rhs, lkeys, rkeys, okey, ps=slice(0, 128), M=128, N=128, start=True, stop=True,
                     l_exact=False, r_exact=False):
                lkeys = list(lkeys)
                rkeys = list(rkeys)
                if os.environ.get("GDN_FP32", "1") == "1":
                    A(PE, lambda e: e.matmul(out, lhsT=lhsT, rhs=rhs, start=start, stop=stop), lkeys + rkeys, [okey])
                    return
                t, lhk = spl["lh"].next()
                lh = t[ps, 0:M]
                A(DVE, lambda e: e.tensor_copy(out=lh, in_=lhsT), lkeys, [lhk])
                if not l_exact:
                    t, llk = spl["ll"].next()
                    ll = t[ps, 0:M]
                    A(DVE, lambda e: e.tensor_tensor(out=ll, in0=lhsT, in1=lh, op=ALU.subtract), lkeys + [lhk], [llk])
                t, rhk = spl["rh"].next()
                rh = t[ps, 0:N]
                A(POOL, lambda e: e.tensor_copy(out=rh, in_=rhs), rkeys, [rhk])
                if not r_exact:
                    t, rlk = spl["rl"].next()
                    rl = t[ps, 0:N]
                    A(POOL, lambda e: e.tensor_tensor(out=rl, in0=rhs, in1=rh, op=ALU.subtract), rkeys + [rhk], [rlk])
                terms = [(lh, lhk, rh, rhk)]
                if not r_exact:
                    terms.append((lh, lhk, rl, rlk))
                if not l_exact:
                    terms.append((ll, llk, rh, rhk))
                if os.environ.get("DBG_NOPE") == "1":
                    return
                for n_, (a_, ak_, b_, bk_) in enumerate(terms):
                    A(PE, lambda e: e.matmul(out, lhsT=a_, rhs=b_, start=(start and n_ == 0),
                                             stop=(stop and n_ == len(terms) - 1)), [ak_, bk_], [okey])

            GIDX = {"gc": 0, "gl": 1, "egc": 2, "ekd": 3, "nb": 4, "bq": 5, "ngc": 6}

            def smv(nm, h):
                return sm[:, GIDX[nm], h:h + 1]

            for s in range(n_seq):
                A(DVE, lambda e: e.memset(S[:].rearrange("p a b -> p (a b)"), 0.0), [], [("S", h) for h in range(8)])
                for b in range(16):
                    i = s * 16 + b
                    cols = slice(i * 128, (i + 1) * 128)
                    if int(os.environ.get('DBG_B2', '9')) <= 0:
                        continue
                    if b >= int(os.environ.get('DBG_NBLK', '16')):
                        continue
                    P.load(qTb[:], qT_d[:, :, cols].rearrange("h d t -> d h t"), "qTb", dram=[("qT", i // 4)])
                    P.load(kTb[:], kT_d[:, :, cols].rearrange("h d t -> d h t"), "kTb", dram=[("kT", i // 4)])
                    P.load(vTb[:], vT_d[:, :, cols].rearrange("h d t -> d h t"), "vTb", dram=[("vT", i // 4)])
                    P.load(bgb[:], bg_d[cols, :], "bgb", dram=[("bg", i)])
                    stage_ctr = [0]
                    new_stage()
                    pq, pqk, pqb, pqo = slot(0)
                    mm32(pqb[:, pqo:pqo + 8], cumU[:], bgb[:, 8:16], ["c_cumU"], ["bgb"], pqk, N=8, l_exact=True)
                    A(ACT, lambda e: e.copy(out=sm[:, 0, :], in_=pqb[:, pqo:pqo + 8]), [pqk], ["sm"])
                    pq, pqk, pqb, pqo = slot(4)
                    mm32(pqb[:, pqo:pqo + 8], lastU[:], bgb[:, 8:16], ["c_lastU"], ["bgb"], pqk, N=8, l_exact=True)
                    A(ACT, lambda e: e.copy(out=sm[:, 1, :], in_=pqb[:, pqo:pqo + 8]), [pqk], ["sm"])
                    A(ACT, lambda e: e.activation(out=sm[:, 2, :], in_=sm[:, 0, :], func=AF.Exp), ["sm"], ["sm"])
                    A(DVE, lambda e: e.tensor_tensor(out=sm[:, 3, :], in0=sm[:, 1, :], in1=sm[:, 0, :], op=ALU.subtract),
                      ["sm"], ["sm"])
                    A(ACT, lambda e: e.activation(out=sm[:, 3, :], in_=sm[:, 3, :], func=AF.Exp), ["sm"], ["sm"])
                    A(DVE, lambda e: e.tensor_scalar(out=sm[:, 4, :], in0=bgb[:, 0:8], scalar1=-1.0, scalar2=None,
                                                     op0=ALU.mult), ["bgb"], ["sm"])
                    A(DVE, lambda e: e.tensor_tensor(out=sm[:, 5, :], in0=bgb[:, 0:8], in1=sm[:, 2, :], op=ALU.mult),
                      ["bgb", "sm"], ["sm"])
                    A(DVE, lambda e: e.tensor_scalar(out=sm[:, 6, :], in0=sm[:, 0, :], scalar1=-1.0, scalar2=None,
                                                     op0=ALU.mult), ["sm"], ["sm"])

                    def hk(nm, h):
                        return (nm, h)

                    def T_(nm, h):
                        return H[nm][:, h, :]

                    def evac(eng, nm, h, src, srck):
                        if eng == ACT:
                            A(ACT, lambda e: e.copy(out=T_(nm, h), in_=src), [srck], [hk(nm, h)])
                        else:
                            A(DVE, lambda e: e.tensor_copy(out=T_(nm, h), in_=src), [srck], [hk(nm, h)])

                    def stage(mm_fn, ev_fn):
                        stage_ctr[0] += 1
                        if stage_ctr[0] > int(os.environ.get("DBG_NST", "99")):
                            return
                        if str(stage_ctr[0]) in os.environ.get("DBG_SKIP", "").split(","):
                            return
                        new_stage()
                        for hg in range(2):
                            for h in range(hg * 4, hg * 4 + 4):
                                mm_fn(h, *slot(h))
                            for h in range(hg * 4, hg * 4 + 4):
                                ev_fn(h, *slot(h))

                    stage(lambda h, pq, pqk, pqb, pqo: mm32(pq, kTb[:, h, :], identf[:], ["kTb"], ["identf"], pqk, r_exact=True),
                          lambda h, pq, pqk, pqb, pqo: evac(ACT, "ktok", h, pq, pqk))
                    stage(lambda h, pq, pqk, pqb, pqo: mm32(pq, vTb[:, h, :], identf[:], ["vTb"], ["identf"], pqk, r_exact=True),
                          lambda h, pq, pqk, pqb, pqo: evac(DVE, "vtok", h, pq, pqk))
                    for h in range(8):
                        A(DVE, lambda e: e.tensor_scalar(out=T_("Gb", h), in0=onesf[:], scalar1=bgb[:, 8 + h:9 + h],
                                                         scalar2=None, op0=ALU.mult), ["onesf", "bgb"], [hk("Gb", h)])
                    stage(lambda h, pq, pqk, pqb, pqo: mm32(pqb[:, pqo:pqo + 2], T_("Gb", h), lastU[:, 0:128:64], [hk("Gb", h)],
                                                             ["c_lastU"], pqk, N=2, r_exact=True),
                          lambda h, pq, pqk, pqb, pqo: A(ACT, lambda e: e.activation(out=cdb[:, h, :], in_=pqb[:, pqo:pqo + 2],
                                                                                       func=AF.Exp), [pqk], [("cdb", h)]))

                    def ev_gc(h, pq, pqk, pqb, pqo):
                        evac(ACT, "GCr", h, pq, pqk)
                        gk = hk("GCr", h)
                        gr = T_("GCr", h)
                        A(DVE, lambda e: e.scalar_tensor_tensor(out=T_("tmp", h), in0=gr, scalar=-1.0, in1=negL[:],
                                                                op0=ALU.mult, op1=ALU.add), [gk, "c_negL"], [hk("tmp", h)])
                        A(ACT, lambda e: e.activation(out=T_("Dm", h), in_=T_("tmp", h), func=AF.Exp, bias=smv("gc", h)),
                          [hk("tmp", h), "sm"], [hk("Dm", h)])
                        A(DVE, lambda e: e.tensor_tensor(out=T_("tmp", h), in0=gr, in1=negLT[:], op=ALU.add),
                          [gk, "c_negLT", hk("Dm", h)], [hk("tmp", h)])
                        A(ACT, lambda e: e.activation(out=T_("DT", h), in_=T_("tmp", h), func=AF.Exp, bias=smv("ngc", h)),
                          [hk("tmp", h), "sm"], [hk("DT", h)])
                        A(ACT, lambda e: e.activation(out=T_("EG", h), in_=gr, func=AF.Exp), [gk], [hk("EG", h)])
                        A(DVE, lambda e: e.tensor_tensor(out=T_("Dm", h), in0=T_("Dm", h), in1=mstrict[:], op=ALU.mult),
                          [hk("Dm", h), "c_mstrict"], [hk("Dm", h)])

                    stage(lambda h, pq, pqk, pqb, pqo: mm32(pq, T_("Gb", h), cumU[:], [hk("Gb", h)], ["c_cumU"], pqk, r_exact=True),
                          ev_gc)
                    A(ACT, lambda e: e.copy(out=H["kT2"][:].rearrange("p a b -> p (a b)"),
                                            in_=kTb[:].rearrange("p a b -> p (a b)")), ["kTb"], [hk("kT2", h) for h in range(8)])
                    stage((lambda h, pq, pqk, pqb, pqo: mm32(pq, kTb[:, h, :], identf[:], ["kTb"], ["identf"], pqk, r_exact=True))
                          if os.environ.get("DBG_RHS") == "ident" else
                          (lambda h, pq, pqk, pqb, pqo: mm32(pq, kTb[:, h, :], T_("kT2", h), ["kTb"], [hk("kT2", h)], pqk)),
                          (lambda h, pq, pqk, pqb, pqo: A(DVE, lambda e: e.tensor_copy(out=T_(os.environ.get("DBG_BUF", "Pm"), h), in_=pq), [pqk], [hk(os.environ.get("DBG_BUF", "Pm"), h)]))
                          if os.environ.get("DBG_V") == "1" else
                          (lambda h, pq, pqk, pqb, pqo: A(DVE, lambda e: e.scalar_tensor_tensor(
                              out=T_("Pm", h), in0=pq, scalar=smv("nb", h), in1=T_("Dm", h), op0=ALU.mult, op1=ALU.mult),
                              [pqk, "sm", hk("Dm", h)], [hk("Pm", h)])))

                    def ev_pt(h, pq, pqk, pqb, pqo):
                        evac(ACT, "PT", h, pq, pqk)
                        A(DVE, lambda e: e.tensor_tensor(out=T_("TT", h), in0=T_("PT", h), in1=identf[:], op=ALU.add),
                          [hk("PT", h), "identf"], [hk("TT", h)])

                    stage(lambda h, pq, pqk, pqb, pqo: mm32(pq, T_("Pm", h), identf[:], [hk("Pm", h)], ["identf"], pqk, r_exact=True),
                          ev_pt)
                    cur, curT, nxt, nxtT = "Pm", "PT", "P2", "P2T"
                    for lvl in range(5):
                        stage(lambda h, pq, pqk, pqb, pqo: mm32(pq, T_(curT, h), T_(cur, h), [hk(curT, h)], [hk(cur, h)], pqk),
                              lambda h, pq, pqk, pqb, pqo: evac(ACT, nxt, h, pq, pqk))
                        if lvl < 4:
                            stage(lambda h, pq, pqk, pqb, pqo: mm32(pq, T_(cur, h), T_(curT, h), [hk(cur, h)], [hk(curT, h)], pqk),
                                  lambda h, pq, pqk, pqb, pqo: evac(ACT, nxtT, h, pq, pqk))
                        stage(lambda h, pq, pqk, pqb, pqo: mm32(pq, T_(nxt, h), T_("TT", h), [hk(nxt, h)], [hk("TT", h)], pqk),
                              lambda h, pq, pqk, pqb, pqo: A(DVE, lambda e: e.tensor_tensor(
                                  out=T_("TT", h), in0=T_("TT", h), in1=pq, op=ALU.add), [pqk, hk("TT", h)], [hk("TT", h)]))
                        cur, curT, nxt, nxtT = nxt, nxtT, cur, curT
                    for h in range(8):
                        A(DVE, lambda e: e.tensor_scalar(out=T_("vb", h), in0=T_("vtok", h), scalar1=bgb[:, h:h + 1],
                                                         scalar2=None, op0=ALU.mult), [hk("vtok", h), "bgb"], [hk("vb", h)])
                        A(DVE, lambda e: e.tensor_scalar(out=T_("kbg", h), in0=T_("ktok", h), scalar1=smv("bq", h),
                                                         scalar2=None, op0=ALU.mult), [hk("ktok", h), "sm"], [hk("kbg", h)])
                        A(DVE, lambda e: e.tensor_scalar(out=T_("kdec", h), in0=T_("ktok", h), scalar1=smv("ekd", h),
                                                         scalar2=None, op0=ALU.mult), [hk("ktok", h), "sm"], [hk("kdec", h)])
                        A(DVE, lambda e: e.tensor_tensor(out=T_("qdT", h), in0=qTb[:, h, :], in1=T_("EG", h), op=ALU.mult),
                          ["qTb", hk("EG", h)], [hk("qdT", h)])
                    stage(lambda h, pq, pqk, pqb, pqo: mm32(pq, T_("TT", h), T_("vb", h), [hk("TT", h)], [hk("vb", h)], pqk),
                          lambda h, pq, pqk, pqb, pqo: evac(ACT, "upre", h, pq, pqk))
                    stage(lambda h, pq, pqk, pqb, pqo: mm32(pq, T_("kbg", h), T_("TT", h), [hk("kbg", h)], [hk("TT", h)], pqk),
                          lambda h, pq, pqk, pqb, pqo: evac(ACT, "wdT", h, pq, pqk))
                    stage(lambda h, pq, pqk, pqb, pqo: mm32(pq, kTb[:, h, :], qTb[:, h, :], ["kTb"], ["qTb"], pqk),
                          lambda h, pq, pqk, pqb, pqo: A(DVE, lambda e: e.tensor_tensor(
                              out=T_("aqkT", h), in0=pq, in1=T_("DT", h), op=ALU.mult), [pqk, hk("DT", h)], [hk("aqkT", h)]))
                    for c in range(2):
                        cs = slice(c * 64, c * 64 + 64)
                        stage(lambda h, pq, pqk, pqb, pqo: mm32(pqb[cs, pqo:pqo + 128], H["wdT"][:, h, cs], S[:, h, :],
                                                                 [hk("wdT", h)], [("S", h)], pqk, M=64),
                              lambda h, pq, pqk, pqb, pqo: A(DVE, lambda e: e.tensor_tensor(
                                  out=H["u"][cs, h, :], in0=H["upre"][cs, h, :], in1=pqb[cs, pqo:pqo + 128], op=ALU.subtract),
                                  [pqk, hk("upre", h)], [hk("u", h)]))

                        def mm_o(h, pq, pqk, pqb, pqo):
                            mm32(pqb[cs, pqo:pqo + 128], H["qdT"][:, h, cs], S[:, h, :], [hk("qdT", h)], [("S", h)], pqk, M=64,
                                 start=True, stop=False)
                            mm32(pqb[cs, pqo:pqo + 128], H["aqkT"][cs, h, cs], H["u"][cs, h, :], [hk("aqkT", h)], [hk("u", h)],
                                 pqk, ps=cs, M=64, start=False, stop=True)

                        stage(mm_o, lambda h, pq, pqk, pqb, pqo: A(ACT, lambda e: e.copy(
                            out=og[cs, h, :], in_=pqb[cs, pqo:pqo + 128]), [pqk], [("og", h)]))
                        stage(lambda h, pq, pqk, pqb, pqo: mm32(pq, H["kdec"][cs, h, :], H["u"][cs, h, :], [hk("kdec", h)],
                                                                 [hk("u", h)], pqk, ps=cs),
                              lambda h, pq, pqk, pqb, pqo: A(DVE, lambda e: e.scalar_tensor_tensor(
                                  out=S[:, h, :], in0=S[:, h, :], scalar=cdb[:, h, c:c + 1], in1=pq, op0=ALU.mult, op1=ALU.add),
                                  [pqk, ("S", h), ("cdb", h)], [("S", h)]))
                    if int(os.environ.get('DBG_B2', '9')) <= 7:
                        continue
                    if int(os.environ.get("DBG_NST", "99")) < 99:
                        continue
                    for h in range(8):
                        A(ACT, lambda e: e.activation(out=junkb[:], in_=og[:, h, :], func=AF.Square,
                                                      accum_out=ss8[:, h:h + 1]), [("og", h)], ["junkb", "ss8"])
                    rstd_from_ss(ss8[:], rs8[:], 1.0 / 128, "ss8", "rs8")
                    for h in range(8):
                        A(DVE, lambda e: e.scalar_tensor_tensor(out=on[:, h * 128:(h + 1) * 128], in0=og[:, h, :],
                                                                scalar=rs8[:, h:h + 1], in1=gob[:], op0=ALU.mult,
                                                                op1=ALU.mult), [("og", h), "rs8", "gob"], ["on"])
                    ev = post_stage(W, li, i, [on[:, 0:512], on[:, 512:1024]], ["on", "on"], xsrc, xdst, last,
                                    pp_ring=ringG)
                    out_events.append(ev)
        P.barrier()
        return out_events


    def build_bias():
        with ExitStack() as es:
            rb = one(es, "rb", [32, 8])
            rb31 = one(es, "rb31", [32, 8])
            P.load(rb[:], dr["rel_bias"], "rb")
            P.load(rb31[:], dr["rel_bias"][31:32, :].partition_broadcast(32), "rb31")
            A(DVE, lambda e: e.tensor_tensor(out=rb[:], in0=rb[:], in1=rb31[:], op=ALU.subtract), ["rb", "rb31"], ["rb"])
            ohs = mkring(es, "ohs", [32, 2048], F32, 2)
            bsb = mkring(es, "bsb", [8, 2048], F32, 2)
            for c4 in range(16):
                oh_t, ohk = ohs.next()
                P.load(oh_t[:], dr["c_oh"][:, c4 * 2048:(c4 + 1) * 2048], ohk)
                bs_t, bsk = bsb.next()
                for q4 in range(4):
                    pb_, pk = ringL.next()
                    A(PE, lambda e, pb_=pb_, oh_t=oh_t, q4=q4: e.matmul(pb_[0:8, :], lhsT=rb[:, :],
                                                                      rhs=oh_t[:, q4 * 512:(q4 + 1) * 512],
                                                                      start=True, stop=True), ["rb", ohk], [pk])
                    A(ACT, lambda e, pb_=pb_, bs_t=bs_t, q4=q4: e.copy(out=bs_t[:, q4 * 512:(q4 + 1) * 512],
                                                                      in_=pb_[0:8, :]), [pk], [bsk])
                P.store(bias_d[:, c4 * 2048:(c4 + 1) * 2048], bs_t[:], bsk, dram=["bias_d"])
        P.barrier()

    if any(li % 2 == 0 for li in layers):
        build_bias()
    evs = []
    cur = dr["x"]
    for n_, li in enumerate(layers):
        last = (n_ == len(layers) - 1)
        dst = y if last else xres
        if li % 2 == 0:
            evs = layer_A(li, li // 2, cur, dst, last)
        else:
            evs = layer_B(li, li // 2, cur, dst, last)
        cur = dst
    P.wait_all(SP, evs)
    if os.environ.get("PROG_STATS"):
        print("PROG_STATS counts", P.count, "streams", {e: len(v) for e, v in P.streams.items()},
              "n dma sems", len(P.dma_sems), flush=True)
    P.emit()
    return nc


_CACHE = {}


def kernel(**inputs):
    n_cores = 8
    consts = host_consts()
    x = np.ascontiguousarray(inputs["x"], dtype=np.float32)
    p = np.ascontiguousarray(inputs["p"], dtype=np.float32)
    nc = build_program(2, (0, 1, 2, 3))
    in_maps = []
    for c in range(n_cores):
        m = {"x": x[2 * c:2 * c + 2].reshape(2 * SEQ, D),
             "p": np.ascontiguousarray(p[:, 2 * c:2 * c + 2].reshape(4, 2 * SEQ, 256))}
        for k in W_SHAPES:
            m[k] = np.ascontiguousarray(inputs[k], dtype=np.float32)
        m.update(consts)
        in_maps.append(m)
    res = run_bass_kernel_spmd(nc, in_maps, core_ids=list(range(n_cores)))
    out = np.concatenate([r["y"].reshape(2, SEQ, D) for r in res.results], axis=0)
    return out.astype(np.float32)
```

```python
import math
import os
from contextlib import ExitStack
import numpy as np
import concourse.bass as bass
import concourse.mybir as mybir
from concourse.bass_utils import run_bass_kernel_spmd

F32 = mybir.dt.float32
BF16 = mybir.dt.bfloat16
ALU = mybir.AluOpType
AF = mybir.ActivationFunctionType

PE, DVE, ACT, POOL, SP = "tensor", "vector", "scalar", "gpsimd", "sync"
ENGS = (PE, DVE, ACT, POOL, SP)

D = 1024
SEQ = 2048
NEG = -1.0e30
EPS = 1e-6


class Prog:
    def __init__(self, nc):
        self.nc = nc
        self.streams = {e: [] for e in ENGS}
        self.count = {e: 0 for e in ENGS}
        self.sem = {e: nc.alloc_semaphore("c_" + e) for e in ENGS}
        self.seen = {e: {} for e in ENGS}
        self.state = {}
        self.dma_sems = {}
        self.n_ops = 0

    def _deps(self, reads, writes):
        deps = []
        for k in reads:
            st = self.state.get(k)
            if st and st[0] is not None:
                deps.append((st[0], True))
        for k in writes:
            st = self.state.get(k)
            if st:
                if st[0] is not None:
                    deps.append((st[0], True))
                for ev in st[1].values():
                    deps.append((ev, False))
        return deps

    def _waits(self, eng, deps):
        best = {}
        for (kind, key, val), raw in deps:
            if kind == "eng" and key == eng and (not raw or eng == PE):
                continue
            sid = (kind, key)
            if val <= self.seen[eng].get(sid, 0):
                continue
            if sid not in best or best[sid] < val:
                best[sid] = val
        for sid, val in best.items():
            self.seen[eng][sid] = val
        return [(k[0], k[1], v) for k, v in best.items()]

    def _commit(self, ev, reads, writes):
        src = (ev[0], ev[1])
        for k in reads:
            st = self.state.setdefault(k, [None, {}])
            st[1][src] = ev
        for k in writes:
            self.state[k] = [ev, {}]

    def op(self, eng, fn, reads=(), writes=()):
        rec = _Rec()
        fn(rec)
        name_, a_, kw_ = rec.call
        fn = lambda e, name_=name_, a_=a_, kw_=kw_: getattr(e, name_)(*a_, **kw_)
        waits = self._waits(eng, self._deps(reads, writes))
        self.count[eng] += 1
        ev = ("eng", eng, self.count[eng])
        self.streams[eng].append((waits, fn, ev))
        self._commit(ev, reads, writes)
        self.n_ops += 1
        return ev

    def _dma(self, eng, out, in_, reads, writes, semkey, kw):
        if semkey not in self.dma_sems:
            self.dma_sems[semkey] = [self.nc.alloc_semaphore("d%d" % len(self.dma_sems)), 0]
        ent = self.dma_sems[semkey]
        waits = self._waits(eng, self._deps(reads, writes))
        ent[1] += 16
        ev = ("dma", semkey, ent[1])
        self.streams[eng].append(
            (waits, lambda e, o=out, i=in_, kw=kw: e.dma_start(out=o, in_=i, **kw), ev))
        self._commit(ev, reads, writes)
        self.n_ops += 1
        return ev

    def load(self, dst, src, sb, dram=(), eng=SP, **kw):
        return self._dma(eng, dst, src, list(dram), [sb], sb, kw)

    def store(self, dst, src, sb, dram=(), eng=POOL, **kw):
        return self._dma(eng, dst, src, [sb], list(dram), sb, kw)

    def wait_all(self, eng, events):
        waits = self._waits(eng, [(ev, True) for ev in events])
        if waits:
            self.streams[eng].append((waits, None, None))

    def barrier(self):
        evs = [("eng", e, self.count[e]) for e in ENGS if self.count[e] > 0]
        evs += [("dma", k, v) for k, (s, v) in self.dma_sems.items() if v > 0]
        for e in ENGS:
            self.wait_all(e, evs)
        self.state = {}

    def emit(self):
        nc = self.nc
        targets = {e: set() for e in ENGS}
        for e in ENGS:
            for waits, fn, ev in self.streams[e]:
                for kind, key, val in waits:
                    if kind == "eng":
                        targets[key].add(val)
        rank = {e: {v: i + 1 for i, v in enumerate(sorted(targets[e]))} for e in ENGS}
        with nc.Block() as block:
            def run(engine_obj, name):
                for waits, fn, ev in self.streams[name]:
                    for kind, key, val in waits:
                        if kind == "eng":
                            engine_obj.wait_ge(self.sem[key], rank[key][val])
                        else:
                            engine_obj.wait_ge(self.dma_sems[key][0], val)
                    if fn is None:
                        continue
                    ins = fn(engine_obj)
                    if ev[0] == "dma":
                        ins.then_inc(self.dma_sems[ev[1]][0], 16)
                    elif ev[2] in targets[name]:
                        ins.then_inc(self.sem[name], 1)

            @block.tensor
            def _(e):
                run(e, PE)

            @block.vector
            def _(e):
                run(e, DVE)

            @block.scalar
            def _(e):
                run(e, ACT)

            @block.gpsimd
            def _(e):
                run(e, POOL)

            @block.sync
            def _(e):
                run(e, SP)


class _Rec:
    def __init__(self):
        self.call = None

    def __getattr__(self, name):
        def f(*a, **kw):
            self.call = (name, a, kw)
            return self
        return f


class Ring:
    def __init__(self, tiles, name):
        self.tiles = tiles
        self.name = name
        self.i = -1

    def next(self):
        self.i = (self.i + 1) % len(self.tiles)
        return self.tiles[self.i], (self.name, self.i)


def _t5_bucket(rel):
    rel = np.maximum(rel, 0)
    rel_f = np.maximum(rel, 1).astype(np.float32)
    log_ratio = np.log(rel_f / np.float32(16)) / np.float32(math.log(128 / 16))
    large = 16 + (log_ratio * np.float32(16)).astype(np.int32)
    large = np.minimum(large, 31)
    return np.where(rel < 16, rel, large)


def host_consts():
    s = np.arange(128)[:, None]
    t = np.arange(128)[None, :]
    oh = np.zeros((32, 2, 128, 128), np.float32)
    for which, off in ((0, 0), (1, 128)):
        rel = off + t - s
        b = _t5_bucket(rel)
        valid = rel >= 0
        for bb in range(32):
            oh[bb, which] = ((b == bb) & valid).astype(np.float32)
    caus01T = (s <= t).astype(np.float32)
    causneg = np.where(t <= s, 0.0, NEG).astype(np.float32)
    chunk = (s // 64) == (t // 64)
    m_strict = (chunk & (s > t)).astype(np.float32)
    m_inclT = (chunk & (s <= t)).astype(np.float32)
    negL = np.where(chunk & (s >= t), 0.0, NEG).astype(np.float32)
    negLT = np.where(chunk & (s <= t), 0.0, NEG).astype(np.float32)
    cumU = (chunk & (s <= t)).astype(np.float32)
    lastsel = (chunk & (s % 64 == 63) & True).astype(np.float32)
    lastU = np.zeros((128, 128), np.float32)
    lastU[:] = chunk.astype(np.float32)
    return {
        "c_ident": np.eye(128, dtype=np.float32),
        "c_oh": oh.reshape(32, 2 * 128 * 128),
        "c_caus01T": caus01T,
        "c_causneg": causneg,
        "c_mstrict": m_strict,
        "c_minclT": m_inclT,
        "c_negL": negL,
        "c_negLT": negLT,
        "c_cumU": cumU,
        "c_lastU": lastU,
    }


W_SHAPES = {
    "norm_w": [4, 1024], "a_w_in": [2, 1024, 1672], "a_g_cq": [2, 256],
    "a_w_uq": [2, 256, 8, 128], "a_w_uk": [2, 8, 128, 256], "a_g_q": [2, 256],
    "a_g_kv": [2, 256], "a_w_iq": [2, 256, 8, 128], "a_w_uv": [2, 8, 256, 128],
    "a_w_out": [2, 1024, 1024], "rel_bias": [32, 8], "b_w_in": [2, 1024, 4112],
    "b_conv_w": [2, 4, 3072], "b_a_log": [2, 8], "b_dt_bias": [2, 8], "b_g_o": [2, 128],
    "b_w_out": [2, 1024, 1024], "ple_norm": [4, 1024], "ple_w_gate": [4, 1024, 1024],
    "ple_w_proj": [4, 256, 1024],
}
C_SHAPES = {
    "c_ident": [128, 128], "c_oh": [32, 32768], "c_caus01T": [128, 128], "c_causneg": [128, 128],
    "c_mstrict": [128, 128], "c_minclT": [128, 128], "c_negL": [128, 128], "c_negLT": [128, 128],
    "c_cumU": [128, 128], "c_lastU": [128, 128],
}


def build_program(n_seq=2, layers=(0, 1, 2, 3), debug=None):
    T = n_seq * SEQ
    NT = T // 128
    nc = bass.Bass("TRN2", target_bir_lowering=False)
    dr = {}
    dr["x"] = nc.dram_tensor("x", [T, D], F32, kind="ExternalInput").ap()
    dr["p"] = nc.dram_tensor("p", [4, T, 256], F32, kind="ExternalInput").ap()
    for k, shp in W_SHAPES.items():
        dr[k] = nc.dram_tensor(k, shp, F32, kind="ExternalInput").ap()
    for k, shp in C_SHAPES.items():
        dr[k] = nc.dram_tensor(k, shp, F32, kind="ExternalInput").ap()
    y = nc.dram_tensor("y", [T, D], F32, kind="ExternalOutput").ap()
    xres = nc.dram_tensor("xres", [T, D], F32, kind="ExternalOutput").ap()
    zs = nc.dram_tensor("zs", [T, D], F32, kind="ExternalOutput").ap()
    cqnT_d = nc.dram_tensor("cqnT_d", [2, 128, T], BF16, kind=("ExternalOutput" if debug and "cqnT_d" in debug else "Internal")).ap()
    ckvT_d = nc.dram_tensor("ckvT_d", [2, 128, T], BF16, kind=("ExternalOutput" if debug and "ckvT_d" in debug else "Internal")).ap()
    kidxT_d = nc.dram_tensor("kidxT_d", [128, T], BF16, kind=("ExternalOutput" if debug and "kidxT_d" in debug else "Internal")).ap()
    ckva_d = nc.dram_tensor("ckva_d", [T, 257], BF16, kind=("ExternalOutput" if debug and "ckva_d" in debug else "Internal")).ap()
    widx_d = nc.dram_tensor("widx_d", [T, 8], F32, kind=("ExternalOutput" if debug and "widx_d" in debug else "Internal")).ap()
    bias_d = nc.dram_tensor("bias_d", [8, 2 * 128 * 128], F32, kind=("ExternalOutput" if debug and "bias_d" in debug else "Internal")).ap()
    qT_d = nc.dram_tensor("qT_d", [8, 128, T], F32, kind="ExternalOutput").ap()
    kT_d = nc.dram_tensor("kT_d", [8, 128, T], F32, kind="ExternalOutput").ap()
    vT_d = nc.dram_tensor("vT_d", [8, 128, T], F32, kind="ExternalOutput").ap()
    bg_d = nc.dram_tensor("bg_d", [T, 16], F32, kind="ExternalOutput").ap()

    P = Prog(nc)
    A = P.op
    dbg_tile = int(os.environ.get("DBG_TILE", "-1"))

    def dump(name, tile, key, shape, dt, i):
        if i != dbg_tile or not debug or name not in debug:
            return
        dd = nc.dram_tensor("dbg_" + name, shape, dt, kind="ExternalOutput").ap()
        P.store(dd, tile, key, dram=["dbg_" + name])

    def sb(name, shape, dt=F32):
        return nc.alloc_sbuf_tensor(name, shape, dt)

    identf = sb("identf", [128, 128])
    identb = sb("identb", [128, 128], BF16)
    epst = sb("epst", [128, 1])
    onesb = sb("onesb", [128, 128], BF16)
    onesf = sb("onesf", [128, 128])
    P.load(identf[:], dr["c_ident"], "identf")
    A(DVE, lambda e: e.tensor_copy(out=identb[:], in_=identf[:]), ["identf"], ["identb"])
    A(DVE, lambda e: e.memset(epst[:], EPS), [], ["epst"])
    A(DVE, lambda e: e.memset(onesb[:], 1.0), [], ["onesb"])
    A(DVE, lambda e: e.memset(onesf[:], 1.0), [], ["onesf"])

    pbank = [nc.alloc_psum_tensor("pb%d" % i, [128, 512], F32) for i in range(6)]
    tbank = [nc.alloc_psum_tensor("tb%d" % i, [128, 1024], BF16) for i in range(2)]

    class BankRing:
        def __init__(self, ids):
            self.ids = ids
            self.i = -1

        def next(self):
            self.i = (self.i + 1) % len(self.ids)
            b = self.ids[self.i]
            return pbank[b], ("pb", b)

    ringG = BankRing([0, 1, 2, 3])
    ringL = BankRing([4, 5])

    class TRing:
        def __init__(self):
            self.i = -1

        def next(self):
            self.i = (self.i + 1) % 2
            return tbank[self.i], ("tb", self.i)

    ringT = TRing()

    uid = [0]

    def mkring(es, name, shape, dt, bufs):
        uid[0] += 1
        tiles = [es.enter_context(nc.sbuf_tensor("%s_%d_u%d" % (name, i, uid[0]), shape, dt)) for i in range(bufs)]
        return Ring(tiles, name)

    def one(es, name, shape, dt=F32):
        uid[0] += 1
        return es.enter_context(nc.sbuf_tensor("%s_u%d" % (name, uid[0]), shape, dt))

    def rstd_from_ss(ss_ap, out_ap, inv_d, rk, wk):
        A(ACT, lambda e: e.activation(out=out_ap, in_=ss_ap, func=AF.Sqrt, scale=inv_d, bias=epst[:, 0:1]),
          [rk, "epst"], [wk])
        A(DVE, lambda e: e.reciprocal(out=out_ap, in_=out_ap), [wk], [wk])

    def load_w(stage, dst, src, n, gain=None, eng=DVE):
        st, k = stage.next()
        P.load(st[:, :n], src, k)
        if gain is not None:
            gap, gk = gain
            A(DVE, lambda e: e.tensor_scalar(out=dst, in0=st[:, :n], scalar1=gap, scalar2=None, op0=ALU.mult),
              [k, gk], ["wts"])
        elif eng == ACT:
            A(ACT, lambda e: e.copy(out=dst, in_=st[:, :n]), [k], ["wts"])
        else:
            A(DVE, lambda e: e.tensor_copy(out=dst, in_=st[:, :n]), [k], ["wts"])

    def alloc_post(es):
        W = {}
        W["wout"] = one(es, "wout", [128, 8, 1024], BF16)
        W["wg"] = one(es, "wg", [128, 8, 1024], BF16)
        W["wp"] = one(es, "wp", [128, 2, 1024], BF16)
        W["plg"] = one(es, "plg", [128, 8])
        W["zt"] = mkring(es, "zt", [128, 1024], F32, 1)
        W["xt2"] = mkring(es, "xt2", [128, 1024], F32, 1)
        W["oz"] = one(es, "oz", [128, 1024], BF16)
        W["ozT"] = one(es, "ozT", [128, 8, 128], BF16)
        W["x1"] = mkring(es, "x1", [128, 1024], F32, 2)
        W["x1b"] = one(es, "x1b", [128, 1024], BF16)
        W["x1T"] = one(es, "x1T", [128, 8, 128], BF16)
        W["gate"] = one(es, "gate", [128, 1024])
        W["pt"] = one(es, "pt", [128, 256])
        W["pb16"] = one(es, "pb16", [128, 256], BF16)
        W["pT"] = one(es, "pTs", [128, 2, 128], BF16)
        W["junk"] = one(es, "junkp", [128, 1024], BF16)
        W["ssp"] = one(es, "ssp", [128, 1])
        W["rsp"] = one(es, "rsp", [128, 1])
        return W

    def load_post_weights(W, stage, li, wout_src):
        P.load(W["plg"][:], dr["ple_norm"][li].rearrange("(c p) -> p c", p=128), "plg",
               allow_slow_non_contiguous=True)
        for kc in range(8):
            load_w(stage, W["wout"][:, kc, :], wout_src[kc * 128:(kc + 1) * 128, :], 1024, eng=ACT)
        for kc in range(8):
            load_w(stage, W["wg"][:, kc, :], dr["ple_w_gate"][li][kc * 128:(kc + 1) * 128, :], 1024,
                   gain=(W["plg"][:, kc:kc + 1], "plg"))
        for kc in range(2):
            load_w(stage, W["wp"][:, kc, :], dr["ple_w_proj"][li][kc * 128:(kc + 1) * 128, :], 1024, eng=ACT)

    def post_stage(W, li, i, o_aps, o_keys, xsrc, xdst, last, pp_ring=None):
        rows = slice(i * 128, (i + 1) * 128)
        zt, zk = W["zt"].next()
        P.load(zt[:], zs[rows, :], zk, dram=[("zs", i)])
        A(ACT, lambda e: e.activation(out=zt[:], in_=zt[:], func=AF.Silu), [zk], [zk])
        for hh in range(2):
            A(DVE, lambda e, hh=hh: e.tensor_tensor(out=W["oz"][:, hh * 512:(hh + 1) * 512], in0=o_aps[hh],
                                                    in1=zt[:, hh * 512:(hh + 1) * 512], op=ALU.mult),
              [o_keys[hh], zk], ["oz"])
        tb, tk = ringT.next()
        for kc in range(8):
            A(PE, lambda e, kc=kc: e.transpose(out=tb[:, kc * 128:(kc + 1) * 128],
                                               in_=W["oz"][:, kc * 128:(kc + 1) * 128], identity=identb[:]),
              ["oz", "identb"], [tk])
        A(ACT, lambda e: e.copy(out=W["ozT"][:].rearrange("p a b -> p (a b)"), in_=tb[:]), [tk], ["ozT"])
        ybanks = [ringG.next(), ringG.next()]
        for hh in range(2):
            pb_, pk = ybanks[hh]
            for kc in range(8):
                A(PE, lambda e, kc=kc, hh=hh, pb_=pb_: e.matmul(pb_[:], lhsT=W["ozT"][:, kc, :],
                                                              rhs=W["wout"][:, kc, hh * 512:(hh + 1) * 512],
                                                              start=(kc == 0), stop=(kc == 7)),
                  ["ozT", "wts"], [pk])
        xt, xk = W["xt2"].next()
        P.load(xt[:], xsrc[rows, :], xk, dram=[("x", i)])
        x1, x1k = W["x1"].next()
        for hh in range(2):
            pb_, pk = ybanks[hh]
            A(DVE, lambda e, hh=hh, pb_=pb_: e.tensor_tensor(out=x1[:, hh * 512:(hh + 1) * 512], in0=pb_[:],
                                                            in1=xt[:, hh * 512:(hh + 1) * 512], op=ALU.add),
              [pk, xk], [x1k])
        A(ACT, lambda e: e.activation(out=W["junk"][:], in_=x1[:], func=AF.Square, accum_out=W["ssp"][:]),
          [x1k], ["junkp", "ssp"])
        rstd_from_ss(W["ssp"][:], W["rsp"][:], 1.0 / D, "ssp", "rsp")
        A(DVE, lambda e: e.tensor_scalar(out=W["x1b"][:], in0=x1[:], scalar1=W["rsp"][:, 0:1], scalar2=None,
                                         op0=ALU.mult), [x1k, "rsp"], ["x1b"])
        tb, tk = ringT.next()
        for kc in range(8):
            A(PE, lambda e, kc=kc: e.transpose(out=tb[:, kc * 128:(kc + 1) * 128],
                                               in_=W["x1b"][:, kc * 128:(kc + 1) * 128], identity=identb[:]),
              ["x1b", "identb"], [tk])
        A(ACT, lambda e: e.copy(out=W["x1T"][:].rearrange("p a b -> p (a b)"), in_=tb[:]), [tk], ["x1T"])
        gbanks = [ringG.next(), ringG.next()]
        for hh in range(2):
            pb_, pk = gbanks[hh]
            for kc in range(8):
                A(PE, lambda e, kc=kc, hh=hh, pb_=pb_: e.matmul(pb_[:], lhsT=W["x1T"][:, kc, :],
                                                              rhs=W["wg"][:, kc, hh * 512:(hh + 1) * 512],
                                                              start=(kc == 0), stop=(kc == 7)),
                  ["x1T", "wts"], [pk])
            A(ACT, lambda e, hh=hh, pb_=pb_: e.activation(out=W["gate"][:, hh * 512:(hh + 1) * 512], in_=pb_[:],
                                                         func=AF.Sigmoid), [pk], ["gate"])
        P.load(W["pt"][:], dr["p"][li][rows, :], "pt")
        A(DVE, lambda e: e.tensor_copy(out=W["pb16"][:], in_=W["pt"][:]), ["pt"], ["pb16"])
        tb, tk = ringT.next()
        for kc in range(2):
            A(PE, lambda e, kc=kc: e.transpose(out=tb[:, kc * 128:(kc + 1) * 128],
                                               in_=W["pb16"][:, kc * 128:(kc + 1) * 128], identity=identb[:]),
              ["pb16", "identb"], [tk])
        A(ACT, lambda e: e.copy(out=W["pT"][:].rearrange("p a b -> p (a b)"), in_=tb[:, 0:256]), [tk], ["pT"])
        for hh in range(2):
            pb_, pk = (pp_ring or ringL).next()
            for kc in range(2):
                A(PE, lambda e, kc=kc, hh=hh, pb_=pb_: e.matmul(pb_[:], lhsT=W["pT"][:, kc, :],
                                                              rhs=W["wp"][:, kc, hh * 512:(hh + 1) * 512],
                                                              start=(kc == 0), stop=(kc == 1)),
                  ["pT", "wts"], [pk])
            A(DVE, lambda e, hh=hh, pb_=pb_: e.tensor_tensor(out=W["gate"][:, hh * 512:(hh + 1) * 512],
                                                            in0=W["gate"][:, hh * 512:(hh + 1) * 512],
                                                            in1=pb_[:], op=ALU.mult), [pk, "gate"], ["gate"])
        A(DVE, lambda e: e.tensor_tensor(out=x1[:], in0=x1[:], in1=W["gate"][:], op=ALU.add),
          [x1k, "gate"], [x1k])
        return P.store(xdst[rows, :], x1[:], x1k, dram=[("x", i)])

    def layer_A(li, j, xsrc, xdst, last):
        out_events = []
        with ExitStack() as es:
            stage = mkring(es, "stage", [128, 1024], F32, 2)
            win = one(es, "win", [128, 8, 1672], BF16)
            gn = one(es, "gn", [128, 8])
            gkvb = one(es, "gkvb", [128, 256])
            P.load(gn[:], dr["norm_w"][li].rearrange("(c p) -> p c", p=128), "gn", allow_slow_non_contiguous=True)
            P.load(gkvb[:], dr["a_g_kv"][j:j + 1, :].partition_broadcast(128), "gkvb")
            for kc in range(8):
                for c0 in (0, 836):
                    load_w(stage, win[:, kc, c0:c0 + 836], dr["a_w_in"][j][kc * 128:(kc + 1) * 128, c0:c0 + 836], 836,
                           gain=(gn[:, kc:kc + 1], "gn"))
            xtr = mkring(es, "xt", [128, 1024], F32, 2)
            junk = one(es, "junk", [128, 1024], BF16)
            ss = one(es, "ss", [128, 4])
            rs = one(es, "rs", [128, 4])
            xb = one(es, "xb", [128, 1024], BF16)
            xTr = mkring(es, "xT", [128, 8, 128], BF16, 2)
            ztr = mkring(es, "zt1", [128, 1024], F32, 2)
            nrm = mkring(es, "nrm", [128, 640], BF16, 2)
            ckvar = mkring(es, "ckva", [128, 257], BF16, 2)
            tps = mkring(es, "tps", [128, 5, 128], BF16, 2)
            widr = mkring(es, "wid", [128, 8], F32, 2)
            for r_ in ckvar.tiles:
                A(DVE, lambda e, r_=r_: e.memset(r_[:, 256:257], 1.0), [], [("ckva", ckvar.tiles.index(r_))])
            for i in range(NT):
                rows = slice(i * 128, (i + 1) * 128)
                xt, xk = xtr.next()
                P.load(xt[:], xsrc[rows, :], xk, dram=[("x", i)])
                A(ACT, lambda e, xt=xt: e.activation(out=junk[:], in_=xt[:], func=AF.Square, accum_out=ss[:, 0:1]),
                  [xk], ["junk", "ss"])
                rstd_from_ss(ss[:, 0:1], rs[:, 0:1], 1.0 / D, "ss", "rs")
                A(DVE, lambda e, xt=xt: e.tensor_scalar(out=xb[:], in0=xt[:], scalar1=rs[:, 0:1], scalar2=None,
                                                       op0=ALU.mult), [xk, "rs"], ["xb"])
                tb, tk = ringT.next()
                for kc in range(8):
                    A(PE, lambda e, kc=kc, tb=tb: e.transpose(out=tb[:, kc * 128:(kc + 1) * 128],
                                                             in_=xb[:, kc * 128:(kc + 1) * 128], identity=identb[:]),
                      ["xb", "identb"], [tk])
                xT, xTk = xTr.next()
                A(ACT, lambda e, xT=xT, tb=tb: e.copy(out=xT[:].rearrange("p a b -> p (a b)"), in_=tb[:]), [tk], [xTk])
                pA, pAk = ringG.next()
                pB, pBk = ringG.next()
                pZ0, pZ0k = ringG.next()
                pZ1, pZ1k = ringG.next()
                for (pb_, pk, c0, n) in ((pA, pAk, 0, 512), (pB, pBk, 512, 136), (pZ0, pZ0k, 648, 512),
                                         (pZ1, pZ1k, 1160, 512)):
                    for kc in range(8):
                        A(PE, lambda e, kc=kc, pb_=pb_, c0=c0, n=n, xT=xT: e.matmul(
                            pb_[:, 0:n], lhsT=xT[:, kc, :], rhs=win[:, kc, c0:c0 + n],
                            start=(kc == 0), stop=(kc == 7)), [xTk, "wts"], [pk])
                zt, zk = ztr.next()
                A(ACT, lambda e, zt=zt, pZ0=pZ0: e.copy(out=zt[:, 0:512], in_=pZ0[:]), [pZ0k], [zk])
                A(ACT, lambda e, zt=zt, pZ1=pZ1: e.copy(out=zt[:, 512:1024], in_=pZ1[:]), [pZ1k], [zk])
                P.store(zs[rows, :], zt[:], zk, dram=[("zs", i)])
                A(ACT, lambda e, pA=pA: e.activation(out=junk[:, 0:256], in_=pA[:, 0:256], func=AF.Square,
                                                     accum_out=ss[:, 1:2]), [pAk], ["junk", "ss"])
                A(ACT, lambda e, pA=pA: e.activation(out=junk[:, 256:512], in_=pA[:, 256:512], func=AF.Square,
                                                     accum_out=ss[:, 2:3]), [pAk], ["junk", "ss"])
                A(ACT, lambda e, pB=pB: e.activation(out=junk[:, 512:640], in_=pB[:, 0:128], func=AF.Square,
                                                     accum_out=ss[:, 3:4]), [pBk], ["junk", "ss"])
                A(ACT, lambda e: e.activation(out=rs[:, 1:3], in_=ss[:, 1:3], func=AF.Sqrt, scale=1.0 / 256,
                                              bias=epst[:, 0:1]), ["ss", "epst"], ["rs"])
                A(ACT, lambda e: e.activation(out=rs[:, 3:4], in_=ss[:, 3:4], func=AF.Sqrt, scale=1.0 / 128,
                                              bias=epst[:, 0:1]), ["ss", "epst"], ["rs"])
                A(DVE, lambda e: e.reciprocal(out=rs[:, 1:4], in_=rs[:, 1:4]), ["rs"], ["rs"])
                nr, nk = nrm.next()
                ca, cak = ckvar.next()
                A(DVE, lambda e, nr=nr, pA=pA: e.tensor_scalar(out=nr[:, 0:256], in0=pA[:, 0:256],
                                                             scalar1=rs[:, 1:2], scalar2=None, op0=ALU.mult),
                  [pAk, "rs"], [nk])
                A(DVE, lambda e, ca=ca, pA=pA: e.scalar_tensor_tensor(out=ca[:, 0:256], in0=pA[:, 256:512],
                                                                    scalar=rs[:, 2:3], in1=gkvb[:],
                                                                    op0=ALU.mult, op1=ALU.mult),
                  [pAk, "rs", "gkvb"], [cak])
                A(DVE, lambda e, nr=nr, pB=pB: e.tensor_scalar(out=nr[:, 512:640], in0=pB[:, 0:128],
                                                             scalar1=rs[:, 3:4], scalar2=None, op0=ALU.mult),
                  [pBk, "rs"], [nk])
                wd, wdk = widr.next()
                A(DVE, lambda e, wd=wd, pB=pB: e.tensor_scalar(out=wd[:], in0=pB[:, 128:136], scalar1=8 ** -0.5,
                                                             scalar2=None, op0=ALU.mult), [pBk], [wdk])
                P.store(widx_d[rows, :], wd[:], wdk, dram=[("widx", i)])
                P.store(ckva_d[rows, :], ca[:], cak, dram=[("ckva", i)])
                tb, tk = ringT.next()
                srcs = [nr[:, 0:128], nr[:, 128:256], ca[:, 0:128], ca[:, 128:256], nr[:, 512:640]]
                for q_, s_ in enumerate(srcs):
                    A(PE, lambda e, q_=q_, s_=s_, tb=tb: e.transpose(out=tb[:, q_ * 128:(q_ + 1) * 128], in_=s_,
                                                                    identity=identb[:]),
                      [nk, cak, "identb"], [tk])
                tp, tpk = tps.next()
                A(ACT, lambda e, tp=tp, tb=tb: e.copy(out=tp[:].rearrange("p a b -> p (a b)"), in_=tb[:, 0:640]),
                  [tk], [tpk])
                P.store(cqnT_d[:, :, rows].rearrange("k p t -> p k t"), tp[:, 0:2, :], tpk, dram=[("cqnT", i)])
                P.store(ckvT_d[:, :, rows].rearrange("k p t -> p k t"), tp[:, 2:4, :], tpk, dram=[("ckvT", i)])
                P.store(kidxT_d[:, rows], tp[:, 4, :], tpk, dram=[("kidxT", i)])
                if "a1" in os.environ.get("DBG_BARRIER", ""):
                    P.barrier()
        P.barrier()

        with ExitStack() as es:
            stage = mkring(es, "stage", [128, 1024], F32, 2)
            W = alloc_post(es)
            load_post_weights(W, stage, li, dr["a_w_out"][j])
            gcq = one(es, "gcq", [128, 2])
            gq = one(es, "gq", [128, 2])
            P.load(gcq[:], dr["a_g_cq"][j].rearrange("(c p) -> p c", p=128), "gcq", allow_slow_non_contiguous=True)
            P.load(gq[:], dr["a_g_q"][j].rearrange("(c p) -> p c", p=128), "gq", allow_slow_non_contiguous=True)
            wuq = one(es, "wuq", [128, 2, 1024], BF16)
            wiq = one(es, "wiq", [128, 2, 1024], BF16)
            wuk = one(es, "wuk", [128, 8, 256], BF16)
            wuv = one(es, "wuv", [128, 8, 2, 128], BF16)
            for kc in range(2):
                load_w(stage, wuq[:, kc, :], dr["a_w_uq"][j][kc * 128:(kc + 1) * 128].rearrange("p h d -> p (h d)"),
                       1024, gain=(gcq[:, kc:kc + 1], "gcq"))
                load_w(stage, wiq[:, kc, :], dr["a_w_iq"][j][kc * 128:(kc + 1) * 128].rearrange("p h d -> p (h d)"),
                       1024, gain=(gcq[:, kc:kc + 1], "gcq"))
            for h0 in (0, 4):
                st, k = stage.next()
                P.load(st[:].rearrange("p (h c) -> p h c", h=4), dr["a_w_uk"][j][h0:h0 + 4].rearrange("h d c -> d h c"), k)
                A(ACT, lambda e, st=st, h0=h0: e.copy(out=wuk[:, h0:h0 + 4, :].rearrange("p h c -> p (h c)"), in_=st[:]),
                  [k], ["wts"])
                st, k = stage.next()
                for hh in range(4):
                    P.load(st[:, hh * 256:(hh + 1) * 256].rearrange("p (k v) -> p k v", k=2),
                           dr["a_w_uv"][j][h0 + hh].rearrange("(k p) v -> p k v", p=128), k)
                A(ACT, lambda e, st=st, h0=h0: e.copy(out=wuv[:, h0:h0 + 4, :, :].rearrange("p h k v -> p (h k v)"),
                                                     in_=st[:]), [k], ["wts"])
            BT = one(es, "BT", [128, 2, 8, 128])
            for w_ in range(2):
                P.load(BT[:, w_, :, :], bias_d.rearrange("h (w s t) -> w s h t", w=2, s=128)[w_], "BT", dram=["bias_d"])
            caus01T = one(es, "caus01T", [128, 128], BF16)
            causneg = one(es, "causneg", [128, 128])
            st, k = stage.next()
            P.load(st[:, 0:128], dr["c_caus01T"], k)
            A(DVE, lambda e, st=st: e.tensor_copy(out=caus01T[:], in_=st[:, 0:128]), [k], ["caus01T"])
            P.load(causneg[:], dr["c_causneg"], "causneg")
            cqnT = one(es, "cqnT", [128, 2, SEQ], BF16)
            ckvT = one(es, "ckvT", [128, 2, SEQ], BF16)
            kidxT = one(es, "kidxT", [128, SEQ], BF16)
            ckva = one(es, "ckva_r", [128, 16, 257], BF16)
            widx = one(es, "widx", [128, 16, 8])
            qnT = one(es, "qnT", [128, 8, 128], BF16)
            qtok = one(es, "qtok", [128, 8, 256], BF16)
            qhT = one(es, "qhT", [128, 2, 8, 128], BF16)
            qiT = one(es, "qiT", [128, 8, 128], BF16)
            ssq = one(es, "ssq", [128, 8])
            qsc = one(es, "qsc", [128, 8])
            junkq = one(es, "junkq", [128, 256], BF16)
            score = one(es, "score", [128, SEQ])
            work = one(es, "work", [128, SEQ])
            mx = one(es, "mx", [128, 8])
            mask01 = one(es, "mask01", [128, SEQ], BF16)
            maskT = one(es, "maskT", [128, 16, 128], BF16)
            relur = mkring(es, "relu", [128, 512], BF16, 3)
            Er = mkring(es, "E", [128, 4, 128], BF16, 3)
            tmpr = mkring(es, "tmpL", [128, 512], F32, 2)
            rden = one(es, "rden", [128, 8])
            olat = one(es, "olat", [128, 8, 256], BF16)
            olT = one(es, "olT", [128, 8, 2, 128], BF16)

            for s in range(n_seq):
                base = s * SEQ
                tl = list(range(s * 16, (s + 1) * 16))
                P.load(cqnT[:], cqnT_d[:, :, base:base + SEQ].rearrange("k p t -> p k t"), "cqnT",
                       dram=[("cqnT", i) for i in tl])
                P.load(ckvT[:], ckvT_d[:, :, base:base + SEQ].rearrange("k p t -> p k t"), "ckvT",
                       dram=[("ckvT", i) for i in tl])
                P.load(kidxT[:], kidxT_d[:, base:base + SEQ], "kidxT", dram=[("kidxT", i) for i in tl])
                P.load(ckva[:], ckva_d[base:base + SEQ, :].rearrange("(n p) c -> p n c", p=128), "ckva_r",
                       dram=[("ckva", i) for i in tl])
                P.load(widx[:], widx_d[base:base + SEQ, :].rearrange("(n p) c -> p n c", p=128), "widx",
                       dram=[("widx", i) for i in tl])
                for qt in range(16):
                    i = s * 16 + qt
                    tcols = slice(qt * 128, (qt + 1) * 128)
                    Sc = (qt + 1) * 128
                    for hg in range(2):
                        pb_, pk = ringG.next()
                        for hh in range(4):
                            h = hg * 4 + hh
                            for kc in range(2):
                                A(PE, lambda e, pb_=pb_, hh=hh, h=h, kc=kc: e.matmul(
                                    pb_[:, hh * 128:(hh + 1) * 128], lhsT=wuq[:, kc, h * 128:(h + 1) * 128],
                                    rhs=cqnT[:, kc, tcols], start=(kc == 0), stop=(kc == 1)),
                                  ["wts", "cqnT"], [pk])
                        A(ACT, lambda e, pb_=pb_, hg=hg: e.copy(
                            out=qnT[:, hg * 4:(hg + 1) * 4, :].rearrange("p a b -> p (a b)"), in_=pb_[:]),
                          [pk], ["qnT"])
                    for hg in range(2):
                        pb_, pk = ringG.next()
                        for hh in range(4):
                            h = hg * 4 + hh
                            for kc in range(2):
                                A(PE, lambda e, pb_=pb_, hh=hh, h=h, kc=kc: e.matmul(
                                    pb_[:, hh * 128:(hh + 1) * 128], lhsT=wiq[:, kc, h * 128:(h + 1) * 128],
                                    rhs=cqnT[:, kc, tcols], start=(kc == 0), stop=(kc == 1)),
                                  ["wts", "cqnT"], [pk])
                        A(ACT, lambda e, pb_=pb_, hg=hg: e.mul(
                            out=qiT[:, hg * 4:(hg + 1) * 4, :].rearrange("p a b -> p (a b)"), in_=pb_[:],
                            mul=128 ** -0.5), [pk], ["qiT"])
                    for h2 in range(4):
                        pb_, pk = ringG.next()
                        for hh in range(2):
                            h = h2 * 2 + hh
                            A(PE, lambda e, pb_=pb_, hh=hh, h=h: e.matmul(
                                pb_[:, hh * 256:(hh + 1) * 256], lhsT=qnT[:, h, :], rhs=wuk[:, h, :],
                                start=True, stop=True), ["qnT", "wts"], [pk])
                            A(ACT, lambda e, pb_=pb_, hh=hh, h=h: e.activation(
                                out=junkq[:], in_=pb_[:, hh * 256:(hh + 1) * 256], func=AF.Square,
                                accum_out=ssq[:, h:h + 1]), [pk], ["junkq", "ssq"])
                        c2 = slice(h2 * 2, h2 * 2 + 2)
                        A(ACT, lambda e, c2=c2: e.activation(out=qsc[:, c2], in_=ssq[:, c2], func=AF.Sqrt,
                                                             scale=1.0 / 256, bias=epst[:, 0:1]),
                          ["ssq", "epst"], ["qsc"])
                        A(DVE, lambda e, c2=c2: e.reciprocal(out=qsc[:, c2], in_=qsc[:, c2]), ["qsc"], ["qsc"])
                        for hh in range(2):
                            h = h2 * 2 + hh
                            A(DVE, lambda e, pb_=pb_, hh=hh, h=h: e.tensor_scalar(
                                out=qtok[:, h, :], in0=pb_[:, hh * 256:(hh + 1) * 256], scalar1=qsc[:, h:h + 1],
                                scalar2=1.0 / 16, op0=ALU.mult, op1=ALU.mult), [pk, "qsc"], ["qtok"])
                    for kc in range(2):
                        tb, tk = ringT.next()
                        for h in range(8):
                            A(PE, lambda e, tb=tb, h=h, kc=kc: e.transpose(
                                out=tb[:, h * 128:(h + 1) * 128], in_=qtok[:, h, kc * 128:(kc + 1) * 128],
                                identity=identb[:]), ["qtok", "identb"], [tk])
                        A(DVE, lambda e, tb=tb, kc=kc: e.tensor_scalar(
                            out=qhT[:, kc, :, :].rearrange("p a b -> p (a b)"), in0=tb[:], scalar1=gq[:, kc:kc + 1],
                            scalar2=None, op0=ALU.mult), [tk, "gq"], ["qhT"])
                    dump("qhT", qhT[:], "qhT", [128, 2, 8, 128], BF16, i)
                    dump("qtok", qtok[:], "qtok", [128, 8, 256], BF16, i)
                    dump("qnT", qnT[:], "qnT", [128, 8, 128], BF16, i)
                    if qt >= 2:
                        for h in range(8):
                            for c0 in range(0, Sc, 512):
                                n = min(512, Sc - c0)
                                pb_, pk = ringL.next()
                                A(PE, lambda e, pb_=pb_, h=h, c0=c0, n=n: e.matmul(
                                    pb_[:, 0:n], lhsT=qiT[:, h, :], rhs=kidxT[:, c0:c0 + n], start=True, stop=True),
                                  ["qiT", "kidxT"], [pk])
                                rl, rk = relur.next()
                                A(ACT, lambda e, pb_=pb_, rl=rl, n=n: e.activation(out=rl[:, 0:n], in_=pb_[:, 0:n],
                                                                                 func=AF.Relu), [pk], [rk])
                                if h == 0:
                                    A(DVE, lambda e, rl=rl, c0=c0, n=n: e.tensor_scalar(
                                        out=score[:, c0:c0 + n], in0=rl[:, 0:n], scalar1=widx[:, qt, 0:1],
                                        scalar2=None, op0=ALU.mult), [rk, "widx"], ["score"])
                                else:
                                    A(DVE, lambda e, rl=rl, c0=c0, n=n, h=h: e.scalar_tensor_tensor(
                                        out=score[:, c0:c0 + n], in0=rl[:, 0:n], scalar=widx[:, qt, h:h + 1],
                                        in1=score[:, c0:c0 + n], op0=ALU.mult, op1=ALU.add),
                                      [rk, "widx", "score"], ["score"])
                        A(DVE, lambda e: e.tensor_tensor(out=score[:, tcols], in0=score[:, tcols], in1=causneg[:],
                                                         op=ALU.add), ["score", "causneg"], ["score"])
                        for r in range(32):
                            src_ = score if r == 0 else work
                            A(DVE, lambda e, src_=src_: e.max(out=mx[:], in_=src_[:, 0:Sc]),
                              ["score", "work"], ["mx"])
                            if r < 31:
                                A(DVE, lambda e, src_=src_: e.match_replace(out=work[:, 0:Sc], in_to_replace=mx[:],
                                                                           in_values=src_[:, 0:Sc], imm_value=NEG),
                                  ["score", "work", "mx"], ["work"])
                        A(DVE, lambda e: e.tensor_scalar(out=mask01[:, 0:Sc], in0=score[:, 0:Sc], scalar1=mx[:, 7:8],
                                                         scalar2=None, op0=ALU.is_ge), ["score", "mx"], ["mask01"])
                        for g0 in range(0, qt + 1, 8):
                            g1 = min(qt + 1, g0 + 8)
                            tb, tk = ringT.next()
                            for kt in range(g0, g1):
                                A(PE, lambda e, tb=tb, kt=kt, g0=g0: e.transpose(
                                    out=tb[:, (kt - g0) * 128:(kt - g0 + 1) * 128],
                                    in_=mask01[:, kt * 128:(kt + 1) * 128], identity=identb[:]),
                                  ["mask01", "identb"], [tk])
                            A(ACT, lambda e, tb=tb, g0=g0, g1=g1: e.copy(
                                out=maskT[:, g0:g1, :].rearrange("p a b -> p (a b)"), in_=tb[:, 0:(g1 - g0) * 128]),
                              [tk], ["maskT"])
                    else:
                        for kt in range(qt + 1):
                            if kt == qt:
                                A(DVE, lambda e, kt=kt: e.tensor_copy(out=maskT[:, kt, :], in_=caus01T[:]),
                                  ["caus01T"], ["maskT"])
                            else:
                                A(DVE, lambda e, kt=kt: e.memset(maskT[:, kt, :], 1.0), [], ["maskT"])
                    for hg in range(2):
                        for kt in range(qt + 1):
                            kcols = slice(kt * 128, (kt + 1) * 128)
                            pL, pLk = ringL.next()
                            for kc in range(2):
                                A(PE, lambda e, pL=pL, kc=kc, kcols=kcols, hg=hg: e.matmul(
                                    pL[:], lhsT=ckvT[:, kc, kcols],
                                    rhs=qhT[:, kc, hg * 4:(hg + 1) * 4, :].rearrange("p a b -> p (a b)"),
                                    start=(kc == 0), stop=(kc == 1)), ["ckvT", "qhT"], [pLk])
                            E, Ek = Er.next()
                            Ef = E[:].rearrange("p a b -> p (a b)")
                            if kt >= qt - 1:
                                w_ = 0 if kt == qt else 1
                                tm, tmk = tmpr.next()
                                A(DVE, lambda e, tm=tm, pL=pL, w_=w_, hg=hg: e.tensor_tensor(
                                    out=tm[:], in0=pL[:],
                                    in1=BT[:, w_, hg * 4:(hg + 1) * 4, :].rearrange("p a b -> p (a b)"), op=ALU.add),
                                  [pLk, "BT"], [tmk])
                                A(ACT, lambda e, tm=tm, Ef=Ef: e.activation(out=Ef, in_=tm[:], func=AF.Exp),
                                  [tmk], [(Ek, q) for q in range(4)])
                            else:
                                A(ACT, lambda e, pL=pL, Ef=Ef: e.activation(out=Ef, in_=pL[:], func=AF.Exp),
                                  [pLk], [(Ek, q) for q in range(4)])
                            for hh in range(4):
                                A(POOL if hh == 3 else DVE, lambda e, E=E, hh=hh, kt=kt: e.tensor_tensor(
                                    out=E[:, hh, :], in0=E[:, hh, :], in1=maskT[:, kt, :], op=ALU.mult),
                                  [(Ek, hh), "maskT"], [(Ek, hh)])
                            for hh in range(4):
                                A(PE, lambda e, E=E, hh=hh, kt=kt: e.matmul(
                                    pbank[hh][:, 0:257], lhsT=E[:, hh, :], rhs=ckva[:, kt, :],
                                    start=(kt == 0), stop=(kt == qt)), [(Ek, hh), "ckva_r"], [("pb", hh)])
                        for hh in range(4):
                            h = hg * 4 + hh
                            A(DVE, lambda e, hh=hh, h=h: e.reciprocal(out=rden[:, h:h + 1], in_=pbank[hh][:, 256:257]),
                              [("pb", hh)], ["rden"])
                            A(DVE, lambda e, hh=hh, h=h: e.tensor_scalar(
                                out=olat[:, h, :], in0=pbank[hh][:, 0:256], scalar1=rden[:, h:h + 1], scalar2=None,
                                op0=ALU.mult), [("pb", hh), "rden"], ["olat"])
                    dump("olat", olat[:], "olat", [128, 8, 256], BF16, i)
                    dump("maskT", maskT[:], "maskT", [128, 16, 128], BF16, i)
                    dump("score", score[:], "score", [128, SEQ], F32, i)
                    dump("mx", mx[:], "mx", [128, 8], F32, i)
                    for kc in range(2):
                        tb, tk = ringT.next()
                        for h in range(8):
                            A(PE, lambda e, tb=tb, h=h, kc=kc: e.transpose(
                                out=tb[:, h * 128:(h + 1) * 128], in_=olat[:, h, kc * 128:(kc + 1) * 128],
                                identity=identb[:]), ["olat", "identb"], [tk])
                        A(ACT, lambda e, tb=tb, kc=kc: e.copy(out=olT[:, :, kc, :],
                                                              in_=tb[:].rearrange("p (a b) -> p a b", a=8)),
                          [tk], ["olT"])
                    obanks = [ringL.next(), ringL.next()]
                    for h in range(8):
                        pb_, pk = obanks[h // 4]
                        for kc in range(2):
                            A(PE, lambda e, pb_=pb_, h=h, kc=kc: e.matmul(
                                pb_[:, (h % 4) * 128:(h % 4 + 1) * 128], lhsT=olT[:, h, kc, :], rhs=wuv[:, h, kc, :],
                                start=(kc == 0), stop=(kc == 1)), ["olT", "wts"], [pk])
                    ev = post_stage(W, li, i, [obanks[0][0][:], obanks[1][0][:]], [obanks[0][1], obanks[1][1]],
                                    xsrc, xdst, last)
                    out_events.append(ev)
                    if "a3" in os.environ.get("DBG_BARRIER", ""):
                        P.barrier()
        P.barrier()
        return out_events

    def layer_B(li, j, xsrc, xdst, last):
        out_events = []
        NB = T // 512
        with ExitStack() as es:
            stage = mkring(es, "stage", [128, 1024], F32, 2)
            win = one(es, "winb", [128, 8, 4112], BF16)
            gn = one(es, "gn", [128, 8])
            P.load(gn[:], dr["norm_w"][li].rearrange("(c p) -> p c", p=128), "gn", allow_slow_non_contiguous=True)
            for kc in range(8):
                for c0 in range(0, 4112, 514):
                    load_w(stage, win[:, kc, c0:c0 + 514], dr["b_w_in"][j][kc * 128:(kc + 1) * 128, c0:c0 + 514],
                           514, gain=(gn[:, kc:kc + 1], "gn"))
            cwr = one(es, "cwr", [96, 128])
            cw = one(es, "cw", [128, 4, 24])
            P.load(cwr[:], dr["b_conv_w"][j].rearrange("w (c p) -> (w c) p", p=128), "cwr")
            pq, pqk = ringL.next()
            A(PE, lambda e: e.transpose(out=pq[:, 0:96], in_=cwr[:], identity=identf[0:96, 0:96]),
              ["cwr", "identf"], [pqk])
            A(ACT, lambda e: e.copy(out=cw[:].rearrange("p w c -> p (w c)"), in_=pq[:, 0:96]), [pqk], ["cw"])
            dtb = one(es, "dtb", [128, 8])
            nea = one(es, "nea", [128, 8])
            one1 = one(es, "one1", [128, 1])
            A(DVE, lambda e: e.memset(one1[:], 1.0), [], ["one1"])
            P.load(dtb[:], dr["b_dt_bias"][j:j + 1, :].partition_broadcast(128), "dtb")
            P.load(nea[:], dr["b_a_log"][j:j + 1, :].partition_broadcast(128), "nea")
            A(ACT, lambda e: e.activation(out=nea[:], in_=nea[:], func=AF.Exp), ["nea"], ["nea"])
            A(DVE, lambda e: e.tensor_scalar(out=nea[:], in0=nea[:], scalar1=-1.0, scalar2=None, op0=ALU.mult),
              ["nea"], ["nea"])
            xtr = mkring(es, "xt", [128, 1024], F32, 2)
            junk = one(es, "junk", [128, 1024], BF16)
            ss = one(es, "ss", [128, 1])
            rs = one(es, "rs", [128, 1])
            xb = one(es, "xb", [128, 1024], BF16)
            xT = one(es, "xTb", [128, 8, 512], BF16)
            ztr = mkring(es, "zt1", [128, 1024], F32, 2)
            bgr = mkring(es, "bgt", [128, 16], F32, 2)
            hal = one(es, "hal", [128, 24, 3])
            rawr = mkring(es, "raw", [128, 515], F32, 4)
            accr = mkring(es, "acc", [128, 512], F32, 4)
            sqr = mkring(es, "sqr", [128, 512], BF16, 4)
            rnr = mkring(es, "rnr", [128, 512], F32, 4)
            outr = mkring(es, "outr", [128, 512], F32, 4)
            dbg_b1 = int(os.environ.get("DBG_B1", "9"))
            for b in range(NB if dbg_b1 >= 2 else 0):
                seq_start = (b % 4 == 0)
                bcols = slice(b * 512, (b + 1) * 512)
                for tt in range(4):
                    i = b * 4 + tt
                    rows = slice(i * 128, (i + 1) * 128)
                    xt, xk = xtr.next()
                    P.load(xt[:], xsrc[rows, :], xk, dram=[("x", i)])
                    A(ACT, lambda e: e.activation(out=junk[:], in_=xt[:], func=AF.Square, accum_out=ss[:, 0:1]),
                      [xk], ["junk", "ss"])
                    rstd_from_ss(ss[:, 0:1], rs[:, 0:1], 1.0 / D, "ss", "rs")
                    A(DVE, lambda e: e.tensor_scalar(out=xb[:], in0=xt[:], scalar1=rs[:, 0:1], scalar2=None,
                                                     op0=ALU.mult), [xk, "rs"], ["xb"])
                    tb, tk = ringT.next()
                    for kc in range(8):
                        A(PE, lambda e: e.transpose(out=tb[:, kc * 128:(kc + 1) * 128],
                                                    in_=xb[:, kc * 128:(kc + 1) * 128], identity=identb[:]),
                          ["xb", "identb"], [tk])
                    A(ACT, lambda e: e.copy(out=xT[:, :, tt * 128:(tt + 1) * 128],
                                            in_=tb[:].rearrange("p (a b) -> p a b", a=8)), [tk], [("xTb", tt)])
                    pZ0, pZ0k = ringG.next()
                    pZ1, pZ1k = ringG.next()
                    pB, pBk = ringG.next()
                    for (pb_, pk, c0, n) in ((pZ0, pZ0k, 3088, 512), (pZ1, pZ1k, 3600, 512), (pB, pBk, 3072, 16)):
                        for kc in range(8):
                            A(PE, lambda e: e.matmul(pb_[:, 0:n], lhsT=xT[:, kc, tt * 128:(tt + 1) * 128],
                                                     rhs=win[:, kc, c0:c0 + n], start=(kc == 0), stop=(kc == 7)),
                              [("xTb", tt), "wts"], [pk])
                    zt, zk = ztr.next()
                    A(ACT, lambda e: e.copy(out=zt[:, 0:512], in_=pZ0[:]), [pZ0k], [zk])
                    A(ACT, lambda e: e.copy(out=zt[:, 512:1024], in_=pZ1[:]), [pZ1k], [zk])
                    P.store(zs[rows, :], zt[:], zk, dram=[("zs", i)])
                    bg, bgk = bgr.next()
                    A(ACT, lambda e: e.activation(out=bg[:, 0:8], in_=pB[:, 0:8], func=AF.Sigmoid), [pBk], [bgk])
                    A(DVE, lambda e: e.tensor_tensor(out=bg[:, 8:16], in0=pB[:, 8:16], in1=dtb[:], op=ALU.add),
                      [pBk, "dtb"], [bgk])
                    A(ACT, lambda e: e.activation(out=bg[:, 8:16], in_=bg[:, 8:16], func=AF.Exp), [bgk], [bgk])
                    A(ACT, lambda e: e.activation(out=bg[:, 8:16], in_=bg[:, 8:16], func=AF.Ln, bias=one1[:, 0:1]),
                      [bgk, "one1"], [bgk])
                    A(DVE, lambda e: e.tensor_tensor(out=bg[:, 8:16], in0=bg[:, 8:16], in1=nea[:], op=ALU.mult),
                      [bgk, "nea"], [bgk])
                    P.store(bg_d[rows, :], bg[:], bgk, dram=[("bg", i)])
                xkeys = [("xTb", q) for q in range(4)]
                for c in range(24 if dbg_b1 >= 3 else 0):
                    pb_, pk = ringL.next()
                    for kc in range(8):
                        A(PE, lambda e: e.matmul(pb_[:], lhsT=win[:, kc, c * 128:(c + 1) * 128], rhs=xT[:, kc, :],
                                                 start=(kc == 0), stop=(kc == 7)), xkeys + ["wts"], [pk])
                    raw, rk = rawr.next()
                    if seq_start:
                        A(DVE, lambda e: e.memset(raw[:, 0:3], 0.0), [], [rk])
                    else:
                        A(DVE, lambda e: e.tensor_copy(out=raw[:, 0:3], in_=hal[:, c, :]), [("hal", c)], [rk])
                    A(ACT, lambda e: e.copy(out=raw[:, 3:515], in_=pb_[:]), [pk], [rk])
                    A(DVE, lambda e: e.tensor_copy(out=hal[:, c, :], in_=raw[:, 512:515]), [rk], [("hal", c)])
                    dbg_c = int(os.environ.get("DBG_B1C", "9"))
                    if dbg_c < 2:
                        continue
                    acc, ak = accr.next()
                    A(DVE, lambda e: e.tensor_scalar(out=acc[:], in0=raw[:, 3:515], scalar1=cw[:, 3, c:c + 1],
                                                     scalar2=None, op0=ALU.mult), [rk, "cw"], [ak])
                    for w_ in (2, 1, 0):
                        eng_ = DVE
                        A(eng_, lambda e: e.scalar_tensor_tensor(out=acc[:], in0=raw[:, w_:w_ + 512],
                                                                 scalar=cw[:, w_, c:c + 1], in1=acc[:],
                                                                 op0=ALU.mult, op1=ALU.add), [rk, "cw", ak], [ak])
                    A(ACT, lambda e: e.activation(out=acc[:], in_=acc[:], func=AF.Silu), [ak], [ak])
                    h = c % 8
                    if dbg_c < 3:
                        continue
                    if c < 16:
                        sq, sk = sqr.next()
                        A(ACT, lambda e: e.activation(out=sq[:], in_=acc[:], func=AF.Square), [ak], [sk])
                        pn, pnk = ringG.next()
                        A(PE, lambda e: e.matmul(pn[:], lhsT=onesb[:], rhs=sq[:], start=True, stop=True),
                          [sk, "onesb"], [pnk])
                        rn, rnk = rnr.next()
                        A(ACT, lambda e: e.activation(out=rn[:], in_=pn[:], func=AF.Sqrt, bias=epst[:, 0:1]),
                          [pnk, "epst"], [rnk])
                        A(DVE, lambda e: e.reciprocal(out=rn[:], in_=rn[:]), [rnk], [rnk])
                        ot, ok_ = outr.next()
                        if c < 8:
                            A(DVE, lambda e: e.scalar_tensor_tensor(out=ot[:], in0=acc[:], scalar=128 ** -0.5,
                                                                    in1=rn[:], op0=ALU.mult, op1=ALU.mult),
                              [ak, rnk], [ok_])
                            P.store(qT_d[h][:, bcols], ot[:], ok_, dram=[("qT", b)])
                        else:
                            A(DVE, lambda e: e.tensor_tensor(out=ot[:], in0=acc[:], in1=rn[:], op=ALU.mult),
                              [ak, rnk], [ok_])
                            P.store(kT_d[h][:, bcols], ot[:], ok_, dram=[("kT", b)])
                    else:
                        P.store(vT_d[h][:, bcols], acc[:], ak, dram=[("vT", b)])
        P.barrier()

        if os.environ.get("DBG_B") == "1":
            return out_events
        with ExitStack() as es:
            stage = mkring(es, "stage", [128, 1024], F32, 2)
            W = alloc_post(es)
            load_post_weights(W, stage, li, dr["b_w_out"][j])
            gob = one(es, "gob", [128, 128])
            P.load(gob[:], dr["b_g_o"][j:j + 1, :].partition_broadcast(128), "gob")
            cst = {}
            for nm in ("c_mstrict", "c_negL", "c_negLT", "c_cumU", "c_lastU"):
                cst[nm] = one(es, nm, [128, 128])
                P.load(cst[nm][:], dr[nm], nm)
            mstrict, negL, negLT, cumU, lastU = (cst[n] for n in ("c_mstrict", "c_negL", "c_negLT", "c_cumU", "c_lastU"))
            qTb = one(es, "qTb", [128, 8, 128])
            kTb = one(es, "kTb", [128, 8, 128])
            vTb = one(es, "vTb", [128, 8, 128])
            bgb = one(es, "bgb", [128, 16])
            sm = one(es, "sm", [128, 8, 8])
            S = one(es, "S", [128, 8, 128])
            og = one(es, "og", [128, 8, 128])
            on = one(es, "on", [128, 1024])
            ss8 = one(es, "ss8", [128, 8])
            rs8 = one(es, "rs8", [128, 8])
            junkb = one(es, "junkb", [128, 128], BF16)
            names = ["ktok", "vtok", "Gb", "tmp", "Dm", "DT", "Pm", "PT", "P2", "P2T", "TT", "vb", "kbg", "upre",
                     "wdT", "aqkT", "EG", "qdT", "kdec", "u", "GCr", "tmp2"]
            H = {nm: one(es, "h_" + nm, [128, 8, 128]) for nm in names}
            cdb = one(es, "cdb", [128, 8, 2])
            phase = [0]

            def new_stage():
                phase[0] ^= 1

            def slot(h):
                bank = (2 if phase[0] else 4) + h // 4
                off = (h % 4) * 128
                return pbank[bank][:, off:off + 128], ("pb", bank), pbank[bank], off

            spl = {nm: mkring(es, "spl_" + nm, [128, 128], BF16, 4) for nm in ("lh", "ll", "rh", "rl")}

            def mm32(out, lhsT, rhs, lkeys, rkeys, okey, ps=slice(0, 128), M=128, N=128, start=True, stop=True,
                     l_exact=False, r_exact=False):
                lkeys = list(lkeys)
                rkeys = list(rkeys)
                if os.environ.get("GDN_FP32", "1") == "1":
                    A(PE, lambda e: e.matmul(out, lhsT=lhsT, rhs=rhs, start=start, stop=stop), lkeys + rkeys, [okey])
                    return
                t, lhk = spl["lh"].next()
                lh = t[ps, 0:M]
                A(DVE, lambda e: e.tensor_copy(out=lh, in_=lhsT), lkeys, [lhk])
                if not l_exact:
                    t, llk = spl["ll"].next()
                    ll = t[ps, 0:M]
                    A(DVE, lambda e: e.tensor_tensor(out=ll, in0=lhsT, in1=lh, op=ALU.subtract), lkeys + [lhk], [llk])
                t, rhk = spl["rh"].next()
                rh = t[ps, 0:N]
                A(POOL, lambda e: e.tensor_copy(out=rh, in_=rhs), rkeys, [rhk])
                if not r_exact:
                    t, rlk = spl["rl"].next()
                    rl = t[ps, 0:N]
                    A(POOL, lambda e: e.tensor_tensor(out=rl, in0=rhs, in1=rh, op=ALU.subtract), rkeys + [rhk], [rlk])
                terms = [(lh, lhk, rh, rhk)]
                if not r_exact:
                    terms.append((lh, lhk, rl, rlk))
                if not l_exact:
                    terms.append((ll, llk, rh, rhk))
                if os.environ.get("DBG_NOPE") == "1":
                    return
                for n_, (a_, ak_, b_, bk_) in enumerate(terms):
                    A(PE, lambda e: e.matmul(out, lhsT=a_, rhs=b_, start=(start and n_ == 0),
                                             stop=(stop and n_ == len(terms) - 1)), [ak_, bk_], [okey])

            GIDX = {"gc": 0, "gl": 1, "egc": 2, "ekd": 3, "nb": 4, "bq": 5, "ngc": 6}

            def smv(nm, h):
                return sm[:, GIDX[nm], h:h + 1]

            for s in range(n_seq):
                A(DVE, lambda e: e.memset(S[:].rearrange("p a b -> p (a b)"), 0.0), [], [("S", h) for h in range(8)])
                for b in range(16):
                    i = s * 16 + b
                    cols = slice(i * 128, (i + 1) * 128)
                    if int(os.environ.get('DBG_B2', '9')) <= 0:
                        continue
                    if b >= int(os.environ.get('DBG_NBLK', '16')):
                        continue
                    P.load(qTb[:], qT_d[:, :, cols].rearrange("h d t -> d h t"), "qTb", dram=[("qT", i // 4)])
                    P.load(kTb[:], kT_d[:, :, cols].rearrange("h d t -> d h t"), "kTb", dram=[("kT", i // 4)])
                    P.load(vTb[:], vT_d[:, :, cols].rearrange("h d t -> d h t"), "vTb", dram=[("vT", i // 4)])
                    P.load(bgb[:], bg_d[cols, :], "bgb", dram=[("bg", i)])
                    stage_ctr = [0]
                    new_stage()
                    pq, pqk, pqb, pqo = slot(0)
                    mm32(pqb[:, pqo:pqo + 8], cumU[:], bgb[:, 8:16], ["c_cumU"], ["bgb"], pqk, N=8, l_exact=True)
                    A(ACT, lambda e: e.copy(out=sm[:, 0, :], in_=pqb[:, pqo:pqo + 8]), [pqk], ["sm"])
                    pq, pqk, pqb, pqo = slot(4)
                    mm32(pqb[:, pqo:pqo + 8], lastU[:], bgb[:, 8:16], ["c_lastU"], ["bgb"], pqk, N=8, l_exact=True)
                    A(ACT, lambda e: e.copy(out=sm[:, 1, :], in_=pqb[:, pqo:pqo + 8]), [pqk], ["sm"])
                    A(ACT, lambda e: e.activation(out=sm[:, 2, :], in_=sm[:, 0, :], func=AF.Exp), ["sm"], ["sm"])
                    A(DVE, lambda e: e.tensor_tensor(out=sm[:, 3, :], in0=sm[:, 1, :], in1=sm[:, 0, :], op=ALU.subtract),
                      ["sm"], ["sm"])
                    A(ACT, lambda e: e.activation(out=sm[:, 3, :], in_=sm[:, 3, :], func=AF.Exp), ["sm"], ["sm"])
                    A(DVE, lambda e: e.tensor_scalar(out=sm[:, 4, :], in0=bgb[:, 0:8], scalar1=-1.0, scalar2=None,
                                                     op0=ALU.mult), ["bgb"], ["sm"])
                    A(DVE, lambda e: e.tensor_tensor(out=sm[:, 5, :], in0=bgb[:, 0:8], in1=sm[:, 2, :], op=ALU.mult),
                      ["bgb", "sm"], ["sm"])
                    A(DVE, lambda e: e.tensor_scalar(out=sm[:, 6, :], in0=sm[:, 0, :], scalar1=-1.0, scalar2=None,
                                                     op0=ALU.mult), ["sm"], ["sm"])

                    def hk(nm, h):
                        return (nm, h)

                    def T_(nm, h):
                        return H[nm][:, h, :]

                    def evac(eng, nm, h, src, srck):
                        if eng == ACT:
                            A(ACT, lambda e: e.copy(out=T_(nm, h), in_=src), [srck], [hk(nm, h)])
                        else:
                            A(DVE, lambda e: e.tensor_copy(out=T_(nm, h), in_=src), [srck], [hk(nm, h)])

                    def stage(mm_fn, ev_fn):
                        stage_ctr[0] += 1
                        if stage_ctr[0] > int(os.environ.get("DBG_NST", "99")):
                            return
                        if str(stage_ctr[0]) in os.environ.get("DBG_SKIP", "").split(","):
                            return
                        new_stage()
                        for hg in range(2):
                            for h in range(hg * 4, hg * 4 + 4):
                                mm_fn(h, *slot(h))
                            for ev_pass in (ev_fn if isinstance(ev_fn, (list, tuple)) else [ev_fn]):
                                for h in range(hg * 4, hg * 4 + 4):
                                    ev_pass(h, *slot(h))

                    stage(lambda h, pq, pqk, pqb, pqo: mm32(pq, kTb[:, h, :], identf[:], ["kTb"], ["identf"], pqk, r_exact=True),
                          lambda h, pq, pqk, pqb, pqo: evac(ACT, "ktok", h, pq, pqk))
                    stage(lambda h, pq, pqk, pqb, pqo: mm32(pq, vTb[:, h, :], identf[:], ["vTb"], ["identf"], pqk, r_exact=True),
                          lambda h, pq, pqk, pqb, pqo: evac(DVE, "vtok", h, pq, pqk))
                    for h in range(8):
                        A(DVE, lambda e: e.tensor_scalar(out=T_("Gb", h), in0=onesf[:], scalar1=bgb[:, 8 + h:9 + h],
                                                         scalar2=None, op0=ALU.mult), ["onesf", "bgb"], [hk("Gb", h)])
                    stage(lambda h, pq, pqk, pqb, pqo: mm32(pqb[:, pqo:pqo + 2], T_("Gb", h), lastU[:, 0:128:64], [hk("Gb", h)],
                                                             ["c_lastU"], pqk, N=2, r_exact=True),
                          lambda h, pq, pqk, pqb, pqo: A(ACT, lambda e: e.activation(out=cdb[:, h, :], in_=pqb[:, pqo:pqo + 2],
                                                                                       func=AF.Exp), [pqk], [("cdb", h)]))

                    def gc1(h, pq, pqk, pqb, pqo):
                        evac(ACT, "GCr", h, pq, pqk)

                    def gc2(h, pq, pqk, pqb, pqo):
                        A(DVE, lambda e: e.scalar_tensor_tensor(out=T_("tmp", h), in0=T_("GCr", h), scalar=-1.0, in1=negL[:],
                                                                op0=ALU.mult, op1=ALU.add), [hk("GCr", h), "c_negL"], [hk("tmp", h)])
                        A(DVE, lambda e: e.tensor_tensor(out=T_("tmp2", h), in0=T_("GCr", h), in1=negLT[:], op=ALU.add),
                          [hk("GCr", h), "c_negLT"], [hk("tmp2", h)])

                    def gc3(h, pq, pqk, pqb, pqo):
                        A(ACT, lambda e: e.activation(out=T_("Dm", h), in_=T_("tmp", h), func=AF.Exp, bias=smv("gc", h)),
                          [hk("tmp", h), "sm"], [hk("Dm", h)])
                        A(ACT, lambda e: e.activation(out=T_("DT", h), in_=T_("tmp2", h), func=AF.Exp, bias=smv("ngc", h)),
                          [hk("tmp2", h), "sm"], [hk("DT", h)])
                        A(ACT, lambda e: e.activation(out=T_("EG", h), in_=T_("GCr", h), func=AF.Exp), [hk("GCr", h)], [hk("EG", h)])

                    def gc4(h, pq, pqk, pqb, pqo):
                        A(DVE, lambda e: e.tensor_tensor(out=T_("Dm", h), in0=T_("Dm", h), in1=mstrict[:], op=ALU.mult),
                          [hk("Dm", h), "c_mstrict"], [hk("Dm", h)])

                    ev_gc = [gc1, gc2, gc3, gc4]

                    stage(lambda h, pq, pqk, pqb, pqo: mm32(pq, T_("Gb", h), cumU[:], [hk("Gb", h)], ["c_cumU"], pqk, r_exact=True),
                          ev_gc)
                    stage((lambda h, pq, pqk, pqb, pqo: mm32(pq, kTb[:, h, :], identf[:], ["kTb"], ["identf"], pqk, r_exact=True))
                          if os.environ.get("DBG_RHS") == "ident" else
                          (lambda h, pq, pqk, pqb, pqo: mm32(pq, kTb[:, h, :], kTb[:, h, :], ["kTb"], ["kTb"], pqk)),
                          (lambda h, pq, pqk, pqb, pqo: A(DVE, lambda e: e.tensor_copy(out=T_(os.environ.get("DBG_BUF", "Pm"), h), in_=pq), [pqk], [hk(os.environ.get("DBG_BUF", "Pm"), h)]))
                          if os.environ.get("DBG_V") == "1" else
                          (lambda h, pq, pqk, pqb, pqo: A(DVE, lambda e: e.scalar_tensor_tensor(
                              out=T_("Pm", h), in0=pq, scalar=smv("nb", h), in1=T_("Dm", h), op0=ALU.mult, op1=ALU.mult),
                              [pqk, "sm", hk("Dm", h)], [hk("Pm", h)])))

                    def ev_pt(h, pq, pqk, pqb, pqo):
                        evac(ACT, "PT", h, pq, pqk)
                        A(DVE, lambda e: e.tensor_tensor(out=T_("TT", h), in0=T_("PT", h), in1=identf[:], op=ALU.add),
                          [hk("PT", h), "identf"], [hk("TT", h)])

                    stage(lambda h, pq, pqk, pqb, pqo: mm32(pq, T_("Pm", h), identf[:], [hk("Pm", h)], ["identf"], pqk, r_exact=True),
                          ev_pt)
                    cur, curT, nxt, nxtT = "Pm", "PT", "P2", "P2T"
                    for lvl in range(5):
                        stage(lambda h, pq, pqk, pqb, pqo: mm32(pq, T_(curT, h), T_(cur, h), [hk(curT, h)], [hk(cur, h)], pqk),
                              lambda h, pq, pqk, pqb, pqo: evac(ACT, nxt, h, pq, pqk))
                        if lvl < 4:
                            stage(lambda h, pq, pqk, pqb, pqo: mm32(pq, T_(cur, h), T_(curT, h), [hk(cur, h)], [hk(curT, h)], pqk),
                                  lambda h, pq, pqk, pqb, pqo: evac(ACT, nxtT, h, pq, pqk))
                        stage(lambda h, pq, pqk, pqb, pqo: mm32(pq, T_(nxt, h), T_("TT", h), [hk(nxt, h)], [hk("TT", h)], pqk),
                              lambda h, pq, pqk, pqb, pqo: A(DVE, lambda e: e.tensor_tensor(
                                  out=T_("TT", h), in0=T_("TT", h), in1=pq, op=ALU.add), [pqk, hk("TT", h)], [hk("TT", h)]))
                        cur, curT, nxt, nxtT = nxt, nxtT, cur, curT
                    for h in range(8):
                        A(DVE, lambda e: e.tensor_scalar(out=T_("vb", h), in0=T_("vtok", h), scalar1=bgb[:, h:h + 1],
                                                         scalar2=None, op0=ALU.mult), [hk("vtok", h), "bgb"], [hk("vb", h)])
                        A(DVE, lambda e: e.tensor_scalar(out=T_("kbg", h), in0=T_("ktok", h), scalar1=smv("bq", h),
                                                         scalar2=None, op0=ALU.mult), [hk("ktok", h), "sm"], [hk("kbg", h)])
                        A(DVE, lambda e: e.tensor_scalar(out=T_("kdec", h), in0=T_("ktok", h), scalar1=smv("ekd", h),
                                                         scalar2=None, op0=ALU.mult), [hk("ktok", h), "sm"], [hk("kdec", h)])
                        A(DVE, lambda e: e.tensor_tensor(out=T_("qdT", h), in0=qTb[:, h, :], in1=T_("EG", h), op=ALU.mult),
                          ["qTb", hk("EG", h)], [hk("qdT", h)])
                    stage(lambda h, pq, pqk, pqb, pqo: mm32(pq, T_("TT", h), T_("vb", h), [hk("TT", h)], [hk("vb", h)], pqk),
                          lambda h, pq, pqk, pqb, pqo: evac(ACT, "upre", h, pq, pqk))
                    stage(lambda h, pq, pqk, pqb, pqo: mm32(pq, T_("kbg", h), T_("TT", h), [hk("kbg", h)], [hk("TT", h)], pqk),
                          lambda h, pq, pqk, pqb, pqo: evac(ACT, "wdT", h, pq, pqk))
                    stage(lambda h, pq, pqk, pqb, pqo: mm32(pq, kTb[:, h, :], qTb[:, h, :], ["kTb"], ["qTb"], pqk),
                          lambda h, pq, pqk, pqb, pqo: A(DVE, lambda e: e.tensor_tensor(
                              out=T_("aqkT", h), in0=pq, in1=T_("DT", h), op=ALU.mult), [pqk, hk("DT", h)], [hk("aqkT", h)]))
                    for c in range(2):
                        cs = slice(c * 64, c * 64 + 64)
                        stage(lambda h, pq, pqk, pqb, pqo: mm32(pqb[cs, pqo:pqo + 128], H["wdT"][:, h, cs], S[:, h, :],
                                                                 [hk("wdT", h)], [("S", h)], pqk, M=64),
                              lambda h, pq, pqk, pqb, pqo: A(DVE, lambda e: e.tensor_tensor(
                                  out=H["u"][cs, h, :], in0=H["upre"][cs, h, :], in1=pqb[cs, pqo:pqo + 128], op=ALU.subtract),
                                  [pqk, hk("upre", h)], [hk("u", h)]))

                        def mm_o(h, pq, pqk, pqb, pqo):
                            mm32(pqb[cs, pqo:pqo + 128], H["qdT"][:, h, cs], S[:, h, :], [hk("qdT", h)], [("S", h)], pqk, M=64,
                                 start=True, stop=False)
                            mm32(pqb[cs, pqo:pqo + 128], H["aqkT"][cs, h, cs], H["u"][cs, h, :], [hk("aqkT", h)], [hk("u", h)],
                                 pqk, ps=cs, M=64, start=False, stop=True)

                        stage(mm_o, lambda h, pq, pqk, pqb, pqo: A(ACT, lambda e: e.copy(
                            out=og[cs, h, :], in_=pqb[cs, pqo:pqo + 128]), [pqk], [("og", h)]))
                        stage(lambda h, pq, pqk, pqb, pqo: mm32(pq, H["kdec"][cs, h, :], H["u"][cs, h, :], [hk("kdec", h)],
                                                                 [hk("u", h)], pqk, ps=cs),
                              lambda h, pq, pqk, pqb, pqo: A(DVE, lambda e: e.scalar_tensor_tensor(
                                  out=S[:, h, :], in0=S[:, h, :], scalar=cdb[:, h, c:c + 1], in1=pq, op0=ALU.mult, op1=ALU.add),
                                  [pqk, ("S", h), ("cdb", h)], [("S", h)]))
                    if int(os.environ.get('DBG_B2', '9')) <= 7:
                        continue
                    if int(os.environ.get("DBG_NST", "99")) < 99:
                        continue
                    for h in range(8):
                        A(ACT, lambda e: e.activation(out=junkb[:], in_=og[:, h, :], func=AF.Square,
                                                      accum_out=ss8[:, h:h + 1]), [("og", h)], ["junkb", "ss8"])
                    rstd_from_ss(ss8[:], rs8[:], 1.0 / 128, "ss8", "rs8")
                    for h in range(8):
                        A(DVE, lambda e: e.scalar_tensor_tensor(out=on[:, h * 128:(h + 1) * 128], in0=og[:, h, :],
                                                                scalar=rs8[:, h:h + 1], in1=gob[:], op0=ALU.mult,
                                                                op1=ALU.mult), [("og", h), "rs8", "gob"], ["on"])
                    ev = post_stage(W, li, i, [on[:, 0:512], on[:, 512:1024]], ["on", "on"], xsrc, xdst, last,
                                    pp_ring=ringG)
                    out_events.append(ev)
        P.barrier()
        return out_events


    def build_bias():
        with ExitStack() as es:
            rb = one(es, "rb", [32, 8])
            rb31 = one(es, "rb31", [32, 8])
            P.load(rb[:], dr["rel_bias"], "rb")
            P.load(rb31[:], dr["rel_bias"][31:32, :].partition_broadcast(32), "rb31")
            A(DVE, lambda e: e.tensor_tensor(out=rb[:], in0=rb[:], in1=rb31[:], op=ALU.subtract), ["rb", "rb31"], ["rb"])
            ohs = mkring(es, "ohs", [32, 2048], F32, 2)
            bsb = mkring(es, "bsb", [8, 2048], F32, 2)
            for c4 in range(16):
                oh_t, ohk = ohs.next()
                P.load(oh_t[:], dr["c_oh"][:, c4 * 2048:(c4 + 1) * 2048], ohk)
                bs_t, bsk = bsb.next()
                for q4 in range(4):
                    pb_, pk = ringL.next()
                    A(PE, lambda e, pb_=pb_, oh_t=oh_t, q4=q4: e.matmul(pb_[0:8, :], lhsT=rb[:, :],
                                                                      rhs=oh_t[:, q4 * 512:(q4 + 1) * 512],
                                                                      start=True, stop=True), ["rb", ohk], [pk])
                    A(ACT, lambda e, pb_=pb_, bs_t=bs_t, q4=q4: e.copy(out=bs_t[:, q4 * 512:(q4 + 1) * 512],
                                                                      in_=pb_[0:8, :]), [pk], [bsk])
                P.store(bias_d[:, c4 * 2048:(c4 + 1) * 2048], bs_t[:], bsk, dram=["bias_d"])
        P.barrier()

    if any(li % 2 == 0 for li in layers):
        build_bias()
    evs = []
    cur = dr["x"]
    for n_, li in enumerate(layers):
        last = (n_ == len(layers) - 1)
        dst = y if last else xres
        if li % 2 == 0:
            evs = layer_A(li, li // 2, cur, dst, last)
        else:
            evs = layer_B(li, li // 2, cur, dst, last)
        cur = dst
    P.wait_all(SP, evs)
    if os.environ.get("PROG_STATS"):
        print("PROG_STATS counts", P.count, "streams", {e: len(v) for e, v in P.streams.items()},
              "n dma sems", len(P.dma_sems), flush=True)
    P.emit()
    return nc


_CACHE = {}


def kernel(**inputs):
    n_cores = 8
    consts = host_consts()
    x = np.ascontiguousarray(inputs["x"], dtype=np.float32)
    p = np.ascontiguousarray(inputs["p"], dtype=np.float32)
    nc = build_program(2, (0, 1, 2, 3))
    in_maps = []
    for c in range(n_cores):
        m = {"x": x[2 * c:2 * c + 2].reshape(2 * SEQ, D),
             "p": np.ascontiguousarray(p[:, 2 * c:2 * c + 2].reshape(4, 2 * SEQ, 256))}
        for k in W_SHAPES:
            m[k] = np.ascontiguousarray(inputs[k], dtype=np.float32)
        m.update(consts)
        in_maps.append(m)
    res = run_bass_kernel_spmd(nc, in_maps, core_ids=list(range(n_cores)))
    out = np.concatenate([r["y"].reshape(2, SEQ, D) for r in res.results], axis=0)
    return out.astype(np.float32)
```
